# Optimizing a Trainium2 kernel written in Bass

```python
import jax, jax.numpy as jnp
from jax import lax
import numpy as np

D_MODEL = 1024
BATCH = 8
SEQ = 4096
DEPTH = 2
DEC_BATCH = 8
DEC_SEQ = 16
PAST_LEN = 4096

CHUNK = 64
N_EVEN = (DEPTH + 1) // 2
N_ODD = DEPTH // 2
RMS_EPS = 1e-6

CONV_DIM = D_MODEL // 2
CONV_WIDTH = 3
N_Q_HEADS = 8
N_KV_HEADS = 2
HEAD_DIM = 64
GQA_GROUP = N_Q_HEADS // N_KV_HEADS
ATTN_DIM = N_Q_HEADS * HEAD_DIM
KV_DIM = N_KV_HEADS * HEAD_DIM
WINDOW = 128
WINDOW_CHUNKS = WINDOW // CHUNK
BAND_KEYS = (WINDOW_CHUNKS + 1) * CHUNK
EVEN_SPLITS = [CONV_DIM, 2 * CONV_DIM, 3 * CONV_DIM, 3 * CONV_DIM + ATTN_DIM, 3 * CONV_DIM + ATTN_DIM + KV_DIM]
EVEN_IN_DIM = 3 * CONV_DIM + ATTN_DIM + 2 * KV_DIM
EVEN_MIX_DIM = CONV_DIM + ATTN_DIM
HGRN_EXPAND = 128
HGRN_HEADS = D_MODEL // HGRN_EXPAND
HGRN_DK = HGRN_EXPAND
HGRN_DV = D_MODEL // HGRN_HEADS
HGRN_BLOCK = 16
PEER_HEADS = 8
PEER_NKEYS = 128
PEER_N_EXPERTS = PEER_NKEYS * PEER_NKEYS
PEER_DKEY = 256
PEER_HALF = PEER_DKEY // 2
PEER_TOPK = 16
PEER_TOKEN_BLOCK = 128

kernel_name = 'hybrid_streaming_conv_swa_hgrn2_peer_step'


def rmsnorm(x, g):
    xf = x.astype(jnp.float32)
    y = xf * lax.rsqrt(jnp.mean(xf * xf, axis=-1, keepdims=True) + RMS_EPS)
    return (y * g.astype(jnp.float32)).astype(x.dtype)


def short_conv(u, buf, w):
    t = u.shape[1]
    up = jnp.concatenate([buf.astype(u.dtype), u], axis=1)
    y = w[0] * up[:, 0:t]
    for j in range(1, CONV_WIDTH):
        y = y + w[j] * up[:, j:j + t]
    return y, up[:, -(CONV_WIDTH - 1):]


def sink_attend(s, v, sinks, eq):
    sk = sinks.astype(jnp.float32).reshape(N_KV_HEADS, GQA_GROUP, 1, 1)
    m = jnp.maximum(jnp.max(s, axis=-1, keepdims=True), sk)
    p = jnp.exp(s - m)
    p = p / (jnp.sum(p, axis=-1, keepdims=True) + jnp.exp(sk - m))
    return jnp.einsum(eq, p.astype(v.dtype), v)


def swa_prompt(q, k, v, sinks):
    b, t = q.shape[:2]
    nc = t // CHUNK
    qc = q.reshape(b, nc, CHUNK, N_KV_HEADS, GQA_GROUP, HEAD_DIM)

    def band(z):
        zc = z.reshape(b, nc, CHUNK, N_KV_HEADS, HEAD_DIM)
        zp = jnp.pad(zc, ((0, 0), (WINDOW_CHUNKS, 0), (0, 0), (0, 0), (0, 0)))
        return jnp.concatenate([zp[:, j:j + nc] for j in range(WINDOW_CHUNKS + 1)], axis=2)

    kb, vb = band(k), band(v)
    key_chunk = jnp.arange(nc)[:, None] + (jnp.arange(BAND_KEYS) // CHUNK)[None, :] - WINDOW_CHUNKS
    valid = key_chunk >= 0
    s = jnp.einsum('bcqkgd,bcskd->bckgqs', qc, kb, preferred_element_type=jnp.float32) * (HEAD_DIM ** -0.5)
    s = jnp.where(valid[None, :, None, None, None, :], s, -jnp.inf)
    o = sink_attend(s, vb, sinks, 'bckgqs,bcskd->bcqkgd')
    return o.reshape(b, t, ATTN_DIM)


def swa_sample(q, k, v, k_cache, v_cache, sinks):
    b, t = q.shape[:2]
    kk = jnp.concatenate([k_cache.astype(k.dtype), k], axis=1)
    vv = jnp.concatenate([v_cache.astype(v.dtype), v], axis=1)
    qg = q.reshape(b, t, N_KV_HEADS, GQA_GROUP, HEAD_DIM)
    s = jnp.einsum('bqkgd,bskd->bkgqs', qg, kk, preferred_element_type=jnp.float32) * (HEAD_DIM ** -0.5)
    o = sink_attend(s, vv, sinks, 'bkgqs,bskd->bqkgd')
    return o.reshape(b, t, ATTN_DIM), kk[:, -WINDOW:], vv[:, -WINDOW:]


def even_mixer(xn, conv_buf, k_cache, v_cache, w_in, w_conv, q_gain, k_gain, sinks, w_out):
    b, t, _ = xn.shape
    z = xn @ w_in
    bg, cg, h, q, k, v = jnp.split(z, EVEN_SPLITS, axis=-1)
    conv_out, new_conv = short_conv(cg * h, conv_buf, w_conv)
    a_out = bg * conv_out
    q = rmsnorm(q.reshape(b, t, N_Q_HEADS, HEAD_DIM), q_gain)
    k = rmsnorm(k.reshape(b, t, N_KV_HEADS, HEAD_DIM), k_gain)
    v = v.reshape(b, t, N_KV_HEADS, HEAD_DIM)
    if k_cache is None:
        o = swa_prompt(q, k, v, sinks)
        new_k, new_v = k[:, -WINDOW:], v[:, -WINDOW:]
    else:
        o, new_k, new_v = swa_sample(q, k, v, k_cache, v_cache, sinks)
    mix = jnp.concatenate([a_out, o], axis=-1) @ w_out
    return mix, new_conv, new_k, new_v


def gla_blocks(q, k, v, logf, s0):
    b, t = q.shape[:2]
    L = HGRN_BLOCK
    n = -(-t // L)
    pad = n * L - t

    def blocks(z):
        zp = jnp.pad(z, ((0, 0), (0, pad), (0, 0), (0, 0)))
        return zp.reshape(b, n, L, z.shape[2], z.shape[3]).transpose(1, 0, 3, 2, 4)

    qb, kb, vb, fb = blocks(q), blocks(k), blocks(v), blocks(logf)
    causal = jnp.tril(jnp.ones((L, L), dtype=bool))

    def step(S, blk):
        qi, ki, vi, fi = blk
        cum = lax.cumsum(fi, axis=2)
        q_t = qi * jnp.exp(cum)
        k_t = ki * jnp.exp(-cum)
        A = jnp.where(causal, jnp.einsum('bhtd,bhsd->bhts', q_t, k_t), 0.0)
        o = jnp.einsum('bhts,bhse->bhte', A, vi) + jnp.einsum('bhtd,bhde->bhte', q_t, S)
        last = cum[:, :, -1:, :]
        S_new = jnp.exp(last[:, :, 0, :])[..., None] * S + jnp.einsum('bhsd,bhse->bhde', ki * jnp.exp(last - cum), vi)
        return S_new, o

    S, ob = lax.scan(step, s0, (qb, kb, vb, fb))
    o = ob.transpose(1, 0, 3, 2, 4).reshape(b, n * L, q.shape[2], v.shape[3])[:, :t]
    return o, S


def hgrn2_mixer(xn, state, w_in, lb_logits, layer, out_gain, w_out):
    b, t, _ = xn.shape
    z = xn @ w_in
    q, f, i, g = jnp.split(z, 4, axis=-1)
    lbs = jax.nn.softmax(lb_logits.astype(jnp.float32), axis=0)
    lbs = jnp.cumsum(lbs, axis=0) - lbs[0]
    lb = lbs[layer]
    fg = lb + (1.0 - lb) * jax.nn.sigmoid(f.astype(jnp.float32))
    logf = jnp.log(fg).reshape(b, t, HGRN_HEADS, HGRN_DK)
    kk = (1.0 - fg).reshape(b, t, HGRN_HEADS, HGRN_DK)
    qq = q.astype(jnp.float32).reshape(b, t, HGRN_HEADS, HGRN_DK)
    vv = i.astype(jnp.float32).reshape(b, t, HGRN_HEADS, HGRN_DV)
    o, S = gla_blocks(qq, kk, vv, logf, state.astype(jnp.float32))
    o = rmsnorm(o.reshape(b, t, D_MODEL).astype(xn.dtype), out_gain) * jax.nn.silu(g)
    return o @ w_out, S.astype(state.dtype)


def peer_ffn(xn, w_query, sub_keys, u_tab, v_tab):
    shape = xn.shape
    xf = xn.reshape(-1, D_MODEL)
    n = xf.shape[0]
    TB = PEER_TOKEN_BLOCK
    nb = -(-n // TB)
    xp = jnp.pad(xf, ((0, nb * TB - n), (0, 0))).reshape(nb, TB, D_MODEL)
    keys = sub_keys.astype(jnp.float32)

    def one_block(xb):
        q = (xb @ w_query).astype(jnp.float32).reshape(TB, PEER_HEADS, 2, PEER_HALF)
        s = jnp.einsum('thpd,hpkd->thpk', q, keys)
        sv, si = lax.top_k(s, PEER_TOPK)
        cand = sv[:, :, 0, :, None] + sv[:, :, 1, None, :]
        cand_idx = si[:, :, 0, :, None] * PEER_NKEYS + si[:, :, 1, None, :]
        cv, ci = lax.top_k(cand.reshape(TB, PEER_HEADS, PEER_TOPK * PEER_TOPK), PEER_TOPK)
        idx = jnp.take_along_axis(cand_idx.reshape(TB, PEER_HEADS, PEER_TOPK * PEER_TOPK), ci, axis=-1)
        gate = jax.nn.softmax(cv, axis=-1)
        u = jnp.take(u_tab, idx, axis=0)
        act = jax.nn.gelu(jnp.einsum('thkd,td->thk', u, xb).astype(jnp.float32), approximate=False)
        vr = jnp.take(v_tab, idx, axis=0)
        return jnp.einsum('thk,thkd->td', (gate * act).astype(xb.dtype), vr)

    y = lax.map(one_block, xp)
    return y.reshape(nb * TB, D_MODEL)[:n].reshape(shape)


def setup_inputs(seed: int = 0) -> dict:
    key = jax.random.key(seed)
    ks = jax.random.split(key, 22)
    f32 = jnp.float32

    def nrm(k, shape, scale):
        return scale * jax.random.normal(k, shape, f32)

    def gain(k, shape):
        return 1.0 + 0.01 * jax.random.normal(k, shape, f32)

    return {
        'x_prompt': nrm(ks[0], (BATCH, SEQ, D_MODEL), 1.0),
        'x_sample': nrm(ks[1], (DEC_BATCH, DEC_SEQ, D_MODEL), 1.0),
        'cache_conv': nrm(ks[2], (N_EVEN, DEC_BATCH, CONV_WIDTH - 1, CONV_DIM), 0.5),
        'cache_k': nrm(ks[3], (N_EVEN, DEC_BATCH, WINDOW, N_KV_HEADS, HEAD_DIM), 1.0),
        'cache_v': nrm(ks[4], (N_EVEN, DEC_BATCH, WINDOW, N_KV_HEADS, HEAD_DIM), 1.0),
        'state_hgrn': nrm(ks[5], (N_ODD, DEC_BATCH, HGRN_HEADS, HGRN_DK, HGRN_DV), 0.5),
        'norm_mix': gain(ks[6], (DEPTH, D_MODEL)),
        'norm_ffn': gain(ks[7], (DEPTH, D_MODEL)),
        'even_w_in': nrm(ks[8], (N_EVEN, D_MODEL, EVEN_IN_DIM), D_MODEL ** -0.5),
        'even_conv_w': nrm(ks[9], (N_EVEN, CONV_WIDTH, CONV_DIM), CONV_WIDTH ** -0.5),
        'even_q_gain': gain(ks[10], (N_EVEN, HEAD_DIM)),
        'even_k_gain': gain(ks[11], (N_EVEN, HEAD_DIM)),
        'even_sinks': nrm(ks[12], (N_EVEN, N_Q_HEADS), 0.5),
        'even_w_out': nrm(ks[13], (N_EVEN, EVEN_MIX_DIM, D_MODEL), EVEN_MIX_DIM ** -0.5),
        'hgrn_w_in': nrm(ks[14], (N_ODD, D_MODEL, 4 * D_MODEL), D_MODEL ** -0.5),
        'hgrn_lb': nrm(ks[15], (DEPTH, D_MODEL), 0.1),
        'hgrn_out_gain': gain(ks[16], (N_ODD, D_MODEL)),
        'hgrn_w_out': nrm(ks[17], (N_ODD, D_MODEL, D_MODEL), D_MODEL ** -0.5),
        'peer_w_query': nrm(ks[18], (DEPTH, D_MODEL, PEER_HEADS * PEER_DKEY), D_MODEL ** -0.5),
        'peer_sub_keys': nrm(ks[19], (DEPTH, PEER_HEADS, 2, PEER_NKEYS, PEER_HALF), PEER_HALF ** -0.5),
        'peer_u': nrm(ks[20], (DEPTH, PEER_N_EXPERTS, D_MODEL), D_MODEL ** -0.5),
        'peer_v': nrm(ks[21], (DEPTH, PEER_N_EXPERTS, D_MODEL), (PEER_HEADS * PEER_TOPK) ** -0.5),
    }


def reference(x_prompt, x_sample, cache_conv, cache_k, cache_v, state_hgrn, norm_mix, norm_ffn,
              even_w_in, even_conv_w, even_q_gain, even_k_gain, even_sinks, even_w_out,
              hgrn_w_in, hgrn_lb, hgrn_out_gain, hgrn_w_out,
              peer_w_query, peer_sub_keys, peer_u, peer_v):
    def run(x, conv_in, k_in, v_in, s_in):
        new_conv, new_k, new_v, new_s = [], [], [], []
        for l in range(DEPTH):
            j = l // 2
            xn = rmsnorm(x, norm_mix[l])
            if l % 2 == 0:
                mix, cb, kb, vb = even_mixer(
                    xn, conv_in[j], None if k_in is None else k_in[j], None if v_in is None else v_in[j],
                    even_w_in[j], even_conv_w[j], even_q_gain[j], even_k_gain[j], even_sinks[j], even_w_out[j])
                new_conv.append(cb)
                new_k.append(kb)
                new_v.append(vb)
            else:
                mix, sb = hgrn2_mixer(xn, s_in[j], hgrn_w_in[j], hgrn_lb, l, hgrn_out_gain[j], hgrn_w_out[j])
                new_s.append(sb)
            x = x + mix
            x = x + peer_ffn(rmsnorm(x, norm_ffn[l]), peer_w_query[l], peer_sub_keys[l], peer_u[l], peer_v[l])
        return x, jnp.stack(new_conv), jnp.stack(new_k), jnp.stack(new_v), jnp.stack(new_s)

    bp = x_prompt.shape[0]
    conv0 = jnp.zeros((N_EVEN, bp, CONV_WIDTH - 1, CONV_DIM), x_prompt.dtype)
    s0 = jnp.zeros((N_ODD, bp, HGRN_HEADS, HGRN_DK, HGRN_DV), x_prompt.dtype)
    y_prompt, conv_p, k_p, v_p, s_p = run(x_prompt, conv0, None, None, s0)
    y_sample, conv_s, k_s, v_s, s_s = run(x_sample, cache_conv, cache_k, cache_v, state_hgrn)
    return (y_prompt, y_sample, conv_p, k_p, v_p, s_p, conv_s, k_s, v_s, s_s)
```

```python
import numpy as np
import ml_dtypes
from contextlib import ExitStack
import concourse.bass as bass
import concourse.mybir as mybir
from concourse.bass_utils import run_bass_kernel_spmd

F32 = mybir.dt.float32
BF = mybir.dt.bfloat16
U32 = mybir.dt.uint32
ALU = mybir.AluOpType
AF = mybir.ActivationFunctionType
AX = mybir.AxisListType

NCORES = 8
NE = 16384
NB = 13
NEG = -30000.0
EPS = 1e-6
QPERM = [0, 4, 1, 5, 2, 6, 3, 7]

C_ID, C_TRI, C_REV, C_S1C, C_S1P, C_S2C, C_S2P = 0, 128, 256, 384, 512, 640, 768
C_BLK, C_IOTA, C_VALID, C_IOTA16, C_N = 896, 898, 914, 916, 932


class Buf:
    __slots__ = ("name", "last_w", "readers")

    def __init__(self, name):
        self.name = name
        self.last_w = None
        self.readers = {}


class Sched:
    ENG = ("pe", "act", "dve", "pool", "sp")

    def __init__(self, nc, stack):
        self.nc = nc
        self.stack = stack
        self.sems = {}
        self.cnt = {}
        self.ops = {e: [] for e in self.ENG}
        self.waited = {e: {} for e in self.ENG}
        self.bufs = {}
        for e in self.ENG:
            self._mksem("E_" + e)
        self.n_ops = 0
        self.rec = None

    def _mksem(self, key):
        self.sems[key] = self.stack.enter_context(self.nc.semaphore(key))
        self.cnt[key] = 0

    def B(self, name):
        b = self.bufs.get(name)
        if b is None:
            b = self.bufs[name] = Buf(name)
        return b

    def _collect(self, eng, reads, writes):
        need = {}

        def add(tok):
            if tok is None:
                return
            k, v = tok
            if eng == "pe" and k == "E_pe":
                return
            if need.get(k, 0) < v:
                need[k] = v

        for b in reads:
            add(b.last_w)
        for b in writes:
            add(b.last_w)
            for k, v in b.readers.items():
                add((k, v))
        waits = []
        wd = self.waited[eng]
        for k, v in need.items():
            if wd.get(k, 0) < v:
                wd[k] = v
                waits.append((k, v))
        return waits

    def _commit(self, tok, reads, writes):
        k, v = tok
        for b in reads:
            if b.readers.get(k, 0) < v:
                b.readers[k] = v
        for b in writes:
            b.last_w = tok
            b.readers = {}

    def op(self, eng, fn, r=(), w=()):
        rec = ("op", eng, fn, tuple(r), tuple(w), None)
        if self.rec is not None:
            self.rec.append(rec)
        else:
            self.process(rec)

    def dma(self, eng, fn, semkey, r=(), w=()):
        rec = ("dma", eng, fn, tuple(r), tuple(w), semkey)
        if self.rec is not None:
            self.rec.append(rec)
        else:
            self.process(rec)

    def mark(self):
        if self.rec is not None:
            self.rec.append(("mark",))

    def process(self, rec):
        if rec[0] == "mark":
            return
        kind, eng, fn, r, w, semkey = rec
        reads = [self.B(x) for x in r]
        writes = [self.B(x) for x in w]
        waits = self._collect(eng, reads, writes)
        if kind == "op":
            key, inc = "E_" + eng, 1
        else:
            key, inc = "D_" + semkey, 16
            if key not in self.sems:
                self._mksem(key)
        self.cnt[key] += inc
        tok = (key, self.cnt[key])
        self.ops[eng].append((waits, fn, key, inc))
        self._commit(tok, reads, writes)
        self.n_ops += 1

    def merge(self, grec, lrec, D=2):
        groups, cur = [], []
        for r_ in grec:
            if r_[0] == "mark":
                if cur:
                    groups.append(cur)
                cur = []
            else:
                cur.append(r_)
        if cur:
            groups.append(cur)
        n = max(1, len(groups))
        cap = max(4, -(-len(lrec) // max(1, int(0.45 * n))))
        last_w, last_r = {}, {}
        assign = []
        g, cnt = 0, 0
        for rec in lrec:
            if rec[0] == "mark":
                assign.append(g)
                continue
            kind, eng, fn, r, w, _ = rec
            req = g
            if kind == "op" and eng == "dve":
                for b in r + w:
                    lw = last_w.get(b)
                    if lw is not None and lw[0] != "dve":
                        req = max(req, lw[1] + (D if lw[0] != "dma" else D + 1))
                for b in w:
                    lr = last_r.get(b)
                    if lr is not None and lr[0] != "dve":
                        req = max(req, lr[1] + D)
            if req > g:
                g, cnt = req, 0
            if cnt >= cap:
                g, cnt = g + 1, 0
            assign.append(g)
            cnt += 1
            pe_ = eng if kind == "op" else "dma"
            for b in r:
                last_r[b] = (pe_, g)
            for b in w:
                last_w[b] = (pe_, g)
                last_r.pop(b, None)
        li = 0
        gi = 0
        while gi < len(groups) or li < len(lrec):
            if gi < len(groups):
                for r_ in groups[gi]:
                    self.process(r_)
            while li < len(lrec) and (assign[li] <= gi or gi >= len(groups)):
                self.process(lrec[li])
                li += 1
            gi += 1
        self.merge_stats = (len(groups), (assign[-1] if assign else 0), len(lrec), cap)

    def wait_all(self, eng):
        bl = list(self.bufs.values())
        waits = self._collect(eng, bl, bl)
        self.ops[eng].append((waits, None, None, 0))

    def emit(self):
        nc, sems, ops = self.nc, self.sems, self.ops

        def run(engine, lst):
            for waits, fn, key, inc in lst:
                for k, v in waits:
                    engine.wait_ge(sems[k], v)
                if fn is not None:
                    fn(engine).then_inc(sems[key], inc)

        with nc.Block() as block:
            @block.tensor
            def _(e):
                run(e, ops["pe"])

            @block.scalar
            def _(e):
                run(e, ops["act"])

            @block.vector
            def _(e):
                run(e, ops["dve"])

            @block.gpsimd
            def _(e):
                run(e, ops["pool"])

            @block.sync
            def _(e):
                run(e, ops["sp"])


WSPEC = [("w_in", 2304), ("w_out0", 1024), ("wq0", 2048), ("hw_in", 4096), ("hw_out", 1024), ("wq1", 2048)]


def chunk_table():
    tab, idx = {}, 0
    for name, n in WSPEC:
        lst = []
        c0 = 0
        while c0 < n:
            wc = min(512, n - c0)
            lst.append((idx, c0, wc))
            idx += 1
            c0 += wc
        tab[name] = lst
    return tab, idx


def build(NT, stage=99):
    nc = bass.Bass("TRN2", target_bir_lowering=False)
    T = NT * 128

    def din(name, shape, dt=F32):
        return nc.dram_tensor(name, shape, dt, kind="ExternalInput").ap()

    def dout(name, shape, dt=F32):
        return nc.dram_tensor(name, shape, dt, kind="ExternalOutput").ap()

    xp_d = din("xp", [T, 1024])
    xs_d = din("xs", [16, 1024])
    cconv_d = din("cconv", [2, 512])
    ck_d = din("ck", [128, 128])
    cv_d = din("cv", [128, 128])
    shg_d = din("shg", [8, 128, 128])
    w_d = {
        "w_in": din("w_in", [1024, 2304]),
        "w_out0": din("w_out0", [1024, 1024]),
        "hw_in": din("hw_in", [1024, 4096]),
        "hw_out": din("hw_out", [1024, 1024]),
    }
    wq_d = din("wq", [2, 1024, 2048])
    w_d["wq0"] = wq_d[0]
    w_d["wq1"] = wq_d[1]
    keysT_d = din("keysT", [2, 128, 2048])
    uv_d = [din("uv0", [NE, 2048]), din("uv1", [NE, 2048])]
    cst_d = din("cst", [128, C_N])
    msk_d = din("msk", [128, 4, 512], BF)
    nffn_d = din("nffn", [128, 2, 1024], BF)
    hlb_d = din("hlb", [128, 2, 1024])
    convw_d = din("convw", [128, 3, 512])
    qkg_d = din("qkg", [128, 640])
    sink_d = din("sinkr", [128, 8])
    gcol_d = din("gcol", [128, 24])

    yp_d = dout("y_p", [T, 1024])
    ys_d = dout("y_s", [16, 1024])
    convp_d = dout("conv_p", [2, 512])
    kp_d = dout("k_p", [128, 128])
    vp_d = dout("v_p", [128, 128])
    sp_d = dout("s_p", [8, 128, 128])
    convs_d = dout("conv_s", [2, 512])
    ks_d = dout("k_s", [128, 128])
    vs_d = dout("v_s", [128, 128])
    ss_d = dout("s_s", [8, 128, 128])

    ctab, nchunks = chunk_table()
    wsc_d = nc.dram_tensor("wsc", [nchunks, 128, 4096], BF, kind="Internal").ap()
    uvb_d = [nc.dram_tensor(f"uvb{l}", [NE, 2048], BF, kind="Internal").ap() for l in range(2)]

    with ExitStack() as st:
        K = Sched(nc, st)

        def sb(name, shape, dt=F32):
            return st.enter_context(nc.sbuf_tensor("sb_" + name, shape, dt))

        PS = [st.enter_context(nc.psum_tensor(f"ps{i}", [128, 512], F32)) for i in range(8)]
        pb = [f"ps{i}" for i in range(8)]

        cst = sb("cst", [128, C_N])
        msk = sb("msk", [128, 4, 512], BF)
        identb = sb("identb", [128, 128], BF)
        nffn = sb("nffn", [128, 2, 1024], BF)
        omlr = sb("omlr", [128, 1024])
        convw = sb("convw", [128, 3, 512])
        qkg = sb("qkg", [128, 640])
        sinkr = sb("sinkr", [128, 8])
        gcol = sb("gcol", [128, 24])
        keysT = sb("keysT", [128, 2, 2048], BF)

        X = [sb(f"X{i}", [128, 1024]) for i in range(3)]
        XN = sb("XN", [128, 1024], BF)
        XT = sb("XT", [128, 8, 128], BF)
        WR = [sb(f"WR{i}", [128, 8, 512], BF) for i in range(3)]
        W = [sb(f"W{i}", [128, 2048]) for i in range(3)]
        MIX = sb("MIX", [128, 1024])
        H = [sb(f"H{i}", [128, 1024], BF) for i in range(4)]
        G = [sb(f"G{i}", [128, 2048], BF) for i in range(NB)]
        DG = [sb(f"DG{i}", [128, 128], BF) for i in range(8)]
        XNBs = [sb(f"XNB{i}", [128, 1024], BF) for i in range(2)]
        U = [sb(f"U{i}", [128, 512]) for i in range(2)]
        QKN = sb("QKN", [128, 640])
        VF = sb("VF", [128, 128])
        KT = [sb(f"KT{i}", [128, 128], BF) for i in range(2)]
        VB = [sb(f"VB{i}", [128, 128], BF) for i in range(2)]
        PB = sb("PB", [128, 8, 256], BF)
        PT = sb("PT", [128, 16, 128], BF)
        S32 = sb("S32", [128, 8, 128])
        SBF = [[sb(f"SBF{p}{b}", [128, 8, 128], BF) for b in range(2)] for p in range(2)]
        CK = sb("CK", [128, 128])
        CV = sb("CV", [128, 128])
        sm = sb("sm", [128, 256])
        SV = sb("SV", [128, 256])
        SI = sb("SI", [128, 256], U32)
        SIF = sb("SIF", [128, 256])
        CVv = sb("CVv", [128, 128])
        CI = sb("CI", [128, 128], U32)
        IF_ = sb("IF", [128, 128])
        JF_ = sb("JF", [128, 128])
        E1 = sb("E1", [128, 128])
        E2 = sb("E2", [128, 128])
        EIDXs = [sb(f"EIDX{i}", [128, 128], U32) for i in range(2)]
        GATEs = [sb(f"GATE{i}", [128, 128]) for i in range(2)]
        APRE = sb("APRE", [128, 128])
        WGT = sb("WGT", [128, 128])
        DEC = sb("DEC", [128, 16])

        ident = cst[:, C_ID:C_ID + 128]
        tri = cst[:, C_TRI:C_TRI + 128]
        rev = cst[:, C_REV:C_REV + 128]
        blkind = cst[:, C_BLK:C_BLK + 2]
        iota16 = cst[:, C_IOTA:C_IOTA + 16]
        validc = cst[:, C_VALID:C_VALID + 1]
        iota16x = cst[:, C_IOTA16:C_IOTA16 + 16]

        def ld(dst, src, key, w):
            K.dma("sp", lambda e: e.dma_start(out=dst, in_=src), key, w=[w])

        ld(cst[:], cst_d, "su0", "cst")
        ld(msk[:], msk_d, "su1", "msk")
        ld(nffn[:], nffn_d, "su2", "nffn")
        ld(W[0][:], hlb_d.rearrange("p a b -> p (a b)"), "su3", "W0")
        ld(convw[:], convw_d, "su4", "convw")
        ld(qkg[:], qkg_d, "su5", "qkg")
        ld(sinkr[:], sink_d, "su6", "sinkr")
        ld(gcol[:], gcol_d, "su7", "gcol")
        K.op("act", lambda e: e.activation(out=identb[:], in_=ident, func=AF.Copy), r=["cst"], w=["identb"])
        K.op("dve", lambda e: e.tensor_tensor(out=omlr[:], in0=W[0][:, 0:1024], in1=W[0][:, 1024:2048], op=ALU.subtract),
             r=["W0"], w=["omlr"])
        K.op("act", lambda e: e.activation(out=omlr[:], in_=omlr[:], func=AF.Sigmoid), r=["omlr"], w=["omlr"])
        K.op("dve", lambda e: e.tensor_scalar(out=qkg[:, 0:512], in0=qkg[:, 0:512], scalar1=0.125, scalar2=None, op0=ALU.mult),
             r=["qkg"], w=["qkg"])
        for l in range(2):
            ld(W[1 + l][:], keysT_d[l], f"su8{l}", f"W{1 + l}")
            K.op("act", lambda e, l=l: e.activation(out=keysT[:, l, :], in_=W[1 + l][:], func=AF.Copy),
                 r=[f"W{1 + l}"], w=["keysT"])

        gsel = {"w_in": 0, "hw_in": 1, "hw_out": 2}
        pi = 0
        si_ = 0
        for name, n in WSPEC:
            for (idx, c0, wc) in ctab[name]:
                slot = pi % 3
                for kh in range(2):
                    stg = W[si_ % 2]
                    stn = f"W{si_ % 2}"
                    src = w_d[name][kh * 512:(kh + 1) * 512, c0:c0 + wc].rearrange("(k p) n -> p k n", p=128)
                    dst = stg[:, 0:4 * wc].rearrange("p (k n) -> p k n", k=4)
                    K.dma("sp", lambda e, dst=dst, src=src: e.dma_start(out=dst, in_=src), f"ps{si_ % 2}", w=[stn])
                    if name in gsel:
                        g = gsel[name]
                        for k4 in range(4):
                            k = kh * 4 + k4
                            K.op("act", lambda e, dst=dst, slot=slot, k=k, k4=k4, wc=wc, g=g: e.activation(
                                out=WR[slot][:, k, 0:wc], in_=dst[:, k4, :], func=AF.Copy,
                                scale=gcol[:, g * 8 + k:g * 8 + k + 1]),
                                r=[stn, "gcol"], w=[f"WR{slot}"])
                    else:
                        K.op("act", lambda e, dst=dst, slot=slot, wc=wc, kh=kh: e.activation(
                            out=WR[slot][:, kh * 4:kh * 4 + 4, 0:wc], in_=dst, func=AF.Copy), r=[stn], w=[f"WR{slot}"])
                    si_ += 1
                K.dma("sp", lambda e, slot=slot, idx=idx, wc=wc: e.dma_start(
                    out=wsc_d[idx].rearrange("p (k n) -> p k n", k=8)[:, :, 0:wc], in_=WR[slot][:, :, 0:wc]),
                    f"pw{slot}", r=[f"WR{slot}"], w=[f"wsc{idx}"])
                pi += 1

        nslab = 0
        for l in range(2):
            srcv = uv_d[l].rearrange("(p r) c -> p r c", p=128)
            dstv = uvb_d[l].rearrange("(p r) c -> p r c", p=128)
            for r_ in range(128):
                a = nslab % 3
                g_ = nslab % NB
                K.dma("sp", lambda e, a=a, r_=r_, srcv=srcv: e.dma_start(out=W[a][:], in_=srcv[:, r_, :]), f"tc{a}", w=[f"W{a}"])
                if nslab % 2 == 0:
                    K.op("act", lambda e, a=a, g_=g_: e.activation(out=G[g_][:], in_=W[a][:], func=AF.Copy), r=[f"W{a}"], w=[f"G{g_}"])
                else:
                    K.op("dve", lambda e, a=a, g_=g_: e.tensor_copy(out=G[g_][:], in_=W[a][:]), r=[f"W{a}"], w=[f"G{g_}"])
                K.dma("pool", lambda e, g_=g_, r_=r_, dstv=dstv: e.dma_start(out=dstv[:, r_, :], in_=G[g_][:]), f"ts{g_}", r=[f"G{g_}"], w=[f"uvb{l}_{g_}"])
                nslab += 1
        wseq = []
        wst = {"issued": 0, "used": 0}

        def wnext():
            i = wst["used"]
            while wst["issued"] < min(i + 3, len(wseq)):
                j = wst["issued"]
                idx = wseq[j][1][0]
                s = j % 3
                wcj = wseq[j][1][2]
                K.dma("sp", lambda e, s=s, idx=idx, wcj=wcj: e.dma_start(
                    out=WR[s][:, :, 0:wcj], in_=wsc_d[idx].rearrange("p (k n) -> p k n", k=8)[:, :, 0:wcj]),
                    f"wl{s}", r=[f"wsc{idx}"], w=[f"WR{s}"])
                wst["issued"] += 1
            wst["used"] += 1
            return i % 3, wseq[i][1]

        dbank = {"i": 0}

        def dense(xT, xTn, name, consume):
            for ci in range(len(ctab[name])):
                s, (idx, c0, wc) = wnext()
                b = 2 + dbank["i"] % 4
                dbank["i"] += 1
                for k in range(8):
                    K.op("pe", lambda e, b=b, k=k, s=s, wc=wc: e.matmul(
                        PS[b][:, 0:wc], lhsT=xT[:, k, :], rhs=WR[s][:, k, 0:wc], start=(k == 0), stop=(k == 7)),
                        r=[xTn, f"WR{s}"], w=[pb[b]])
                consume(ci, c0, wc, b)

        def transpose8(src, srcn, dstT, dstn, bf=False):
            if bf:
                pv0 = PS[0][:, :].bitcast(BF)
                for k in range(8):
                    K.op("pe", lambda e, k=k: e.transpose(pv0[:, k * 128:(k + 1) * 128], src[:, k * 128:(k + 1) * 128], identb[:]),
                         r=[srcn, "identb"], w=[pb[0]])
                K.op("act", lambda e: e.activation(out=dstT[:, :, :].rearrange("p k n -> p (k n)"), in_=pv0, func=AF.Copy), r=[pb[0]], w=[dstn])
                return
            for k in range(8):
                b = k // 4
                K.op("pe", lambda e, k=k, b=b: e.transpose(
                    PS[b][:, (k % 4) * 128:(k % 4 + 1) * 128], src[:, k * 128:(k + 1) * 128], ident),
                    r=[srcn, "cst"], w=[pb[b]])
            for b in range(2):
                K.op("act", lambda e, b=b: e.activation(
                    out=dstT[:, 4 * b:4 * b + 4, :], in_=PS[b][:, :].rearrange("p (k n) -> p k n", k=4), func=AF.Copy),
                    r=[pb[b]], w=[dstn])

        def rmsnorm(xt, xtn, gain, gainn, out, outn):
            K.op("act", lambda e: e.activation(out=W[1][:, 0:1024], in_=xt, func=AF.Square, accum_out=sm[:, 0:1]),
                 r=[xtn], w=["W1", "sm"])
            K.op("act", lambda e: e.activation(out=sm[:, 1:2], in_=sm[:, 0:1], func=AF.Sqrt, scale=1.0 / 1024, bias=EPS),
                 r=["sm"], w=["sm"])
            K.op("dve", lambda e: e.reciprocal(out=sm[:, 2:3], in_=sm[:, 1:2]), r=["sm"], w=["sm"])
            if gain is None:
                K.op("dve", lambda e: e.tensor_scalar(out=out, in0=xt, scalar1=sm[:, 2:3], scalar2=None, op0=ALU.mult),
                     r=[xtn, "sm"], w=[outn])
            else:
                K.op("dve", lambda e: e.scalar_tensor_tensor(out=out, in0=xt, scalar=sm[:, 2:3], in1=gain,
                                                            op0=ALU.mult, op1=ALU.mult),
                     r=[xtn, "sm", gainn], w=[outn])

        def peer_topk(l, Xt, Xn, gp):
            EIDX, GATE, XNB = EIDXs[gp], GATEs[gp], XNBs[gp]
            en, gn, xnbn = f"EIDX{gp}", f"GATE{gp}", f"XNB{gp}"
            rmsnorm(Xt[:], Xn, nffn[:, l, :], "nffn", XNB[:], xnbn)
            pv0 = PS[0][:, :].bitcast(BF)
            for k in range(8):
                K.op("pe", lambda e, k=k: e.transpose(pv0[:, k * 128:(k + 1) * 128], XNB[:, k * 128:(k + 1) * 128], identb[:]),
                     r=[xnbn, "identb"], w=[pb[0]])
            K.op("act", lambda e: e.activation(out=XT[:, :, :].rearrange("p k n -> p (k n)"), in_=pv0, func=AF.Copy), r=[pb[0]], w=["XT"])
            QT = H[0:2]
            for ci in range(4):
                s, (idx, c0, wc) = wnext()
                for j in range(4):
                    blk = ci * 4 + j
                    b = 2 + blk // 4
                    for k in range(8):
                        K.op("pe", lambda e, b=b, j=j, k=k, s=s: e.matmul(
                            PS[b][:, j * 128:(j + 1) * 128], lhsT=WR[s][:, k, j * 128:(j + 1) * 128], rhs=XT[:, k, :],
                            start=(k == 0), stop=(k == 7)), r=["XT", f"WR{s}"], w=[pb[b]])
                b = 2 + ci
                hq = H[ci // 2]
                K.op("act", lambda e, b=b, hq=hq, ci=ci: e.activation(
                    out=hq[:, (ci % 2) * 512:(ci % 2 + 1) * 512], in_=PS[b][:, :], func=AF.Copy),
                    r=[pb[b]], w=[f"H{ci // 2}"])
            sbanks = [0, 1, 2, 3]
            for blk in range(16):
                b = sbanks[blk // 4]
                hq = H[blk // 8]
                K.op("pe", lambda e, b=b, blk=blk, hq=hq: e.matmul(
                    PS[b][:, (blk % 4) * 128:(blk % 4 + 1) * 128],
                    lhsT=hq[:, (blk % 8) * 128:(blk % 8 + 1) * 128],
                    rhs=keysT[:, l, blk * 128:(blk + 1) * 128], start=True, stop=True),
                    r=[f"H{blk // 8}", "keysT"], w=[pb[b]])
            for g in range(4):
                b = sbanks[g]
                K.op("act", lambda e, b=b, g=g: e.activation(out=W[0][:, g * 512:(g + 1) * 512], in_=PS[b][:, :], func=AF.Copy),
                     r=[pb[b]], w=["W0"])
            for blk in range(16):
                sc = W[0][:, blk * 128:(blk + 1) * 128]
                for half in range(2):
                    o8 = blk * 16 + half * 8
                    K.op("dve", lambda e, sc=sc, o8=o8: e.max(out=SV[:, o8:o8 + 8], in_=sc), r=["W0"], w=["SV"])
                    K.op("dve", lambda e, sc=sc, o8=o8: e.max_index(out=SI[:, o8:o8 + 8], in_max=SV[:, o8:o8 + 8], in_values=sc),
                         r=["W0", "SV"], w=["SI"])
                    if half == 0:
                        K.op("dve", lambda e, sc=sc, o8=o8: e.match_replace(out=sc, in_to_replace=SV[:, o8:o8 + 8],
                                                                            in_values=sc, imm_value=-1e30),
                             r=["W0", "SV"], w=["W0"])
            K.op("dve", lambda e: e.tensor_copy(out=SIF[:], in_=SI[:]), r=["SI"], w=["SIF"])
            sv4 = SV[:, :].rearrange("p (h q k) -> p h q k", h=8, q=2)
            cand = W[1][:, :].rearrange("p (h i j) -> p h i j", h=8, i=16)
            K.op("dve", lambda e: e.tensor_tensor(
                out=cand, in0=sv4[:, :, 0, :].unsqueeze(3).broadcast_to([128, 8, 16, 16]),
                in1=sv4[:, :, 1, :].unsqueeze(2).broadcast_to([128, 8, 16, 16]), op=ALU.add),
                r=["SV"], w=["W1"])
            for h in range(8):
                cd = W[1][:, h * 256:(h + 1) * 256]
                for half in range(2):
                    o8 = h * 16 + half * 8
                    K.op("dve", lambda e, cd=cd, o8=o8: e.max(out=CVv[:, o8:o8 + 8], in_=cd), r=["W1"], w=["CVv"])
                    K.op("dve", lambda e, cd=cd, o8=o8: e.max_index(out=CI[:, o8:o8 + 8], in_max=CVv[:, o8:o8 + 8], in_values=cd),
                         r=["W1", "CVv"], w=["CI"])
                    if half == 0:
                        K.op("dve", lambda e, cd=cd, o8=o8: e.match_replace(out=cd, in_to_replace=CVv[:, o8:o8 + 8],
                                                                            in_values=cd, imm_value=-1e30),
                             r=["W1", "CVv"], w=["W1"])
            K.op("dve", lambda e: e.tensor_copy(out=JF_[:], in_=CI[:]), r=["CI"], w=["JF"])
            ge = W[2][:, :].rearrange("p (a i) -> p a i", i=16)
            K.op("dve", lambda e: e.tensor_tensor(
                out=ge, in0=JF_[:, :].unsqueeze(2).broadcast_to([128, 128, 16]),
                in1=iota16x.unsqueeze(1).broadcast_to([128, 128, 16]), op=ALU.is_ge), r=["JF", "cst"], w=["W2"])
            K.op("dve", lambda e: e.tensor_reduce(out=IF_[:], in_=ge, axis=AX.X, op=ALU.add), r=["W2"], w=["IF"])
            K.op("dve", lambda e: e.tensor_scalar(out=IF_[:], in0=IF_[:], scalar1=-1.0, scalar2=None, op0=ALU.add), r=["IF"], w=["IF"])
            K.op("dve", lambda e: e.scalar_tensor_tensor(out=JF_[:], in0=IF_[:], scalar=-16.0, in1=JF_[:], op0=ALU.mult, op1=ALU.add),
                 r=["IF", "JF"], w=["JF"])
            sif4 = SIF[:, :].rearrange("p (h q k) -> p h q k", h=8, q=2)
            for (src, q, dst, dn) in ((IF_, 0, E1, "E1"), (JF_, 1, E2, "E2")):
                eq = W[2][:, :].rearrange("p (a i) -> p a i", i=16)
                K.op("dve", lambda e, src=src, eq=eq: e.tensor_tensor(
                    out=eq, in0=iota16.unsqueeze(1).broadcast_to([128, 128, 16]),
                    in1=src[:, :].unsqueeze(2).broadcast_to([128, 128, 16]), op=ALU.is_equal),
                    r=["cst", "IF", "JF"], w=["W2"])
                eq4 = W[2][:, :].rearrange("p (h k i) -> p h k i", h=8, k=16)
                K.op("dve", lambda e, eq4=eq4, q=q: e.tensor_tensor(
                    out=eq4, in0=eq4, in1=sif4[:, :, q, :].unsqueeze(2).broadcast_to([128, 8, 16, 16]), op=ALU.mult),
                    r=["W2", "SIF"], w=["W2"])
                K.op("dve", lambda e, eq=eq, dst=dst: e.tensor_reduce(out=dst[:], in_=eq, axis=AX.X, op=ALU.add),
                     r=["W2"], w=[dn])
            K.op("dve", lambda e: e.scalar_tensor_tensor(out=E1[:], in0=E1[:], scalar=128.0, in1=E2[:], op0=ALU.mult, op1=ALU.add),
                 r=["E1", "E2"], w=["E1"])
            K.op("dve", lambda e: e.tensor_copy(out=EIDX[:], in_=E1[:]), r=["E1"], w=[en])
            cv3 = CVv[:, :].rearrange("p (h k) -> p h k", h=8)
            g3 = GATE[:, :].rearrange("p (h k) -> p h k", h=8)
            K.op("dve", lambda e: e.tensor_tensor(out=g3, in0=cv3, in1=cv3[:, :, 0:1].broadcast_to([128, 8, 16]), op=ALU.subtract),
                 r=["CVv"], w=[gn])
            K.op("act", lambda e: e.activation(out=GATE[:], in_=GATE[:], func=AF.Exp), r=[gn], w=[gn])
            K.op("dve", lambda e: e.tensor_reduce(out=sm[:, 8:16], in_=g3, axis=AX.X, op=ALU.add), r=[gn], w=["sm"])
            K.op("dve", lambda e: e.reciprocal(out=sm[:, 16:24], in_=sm[:, 8:16]), r=["sm"], w=["sm"])
            K.op("dve", lambda e: e.tensor_tensor(out=g3, in0=g3, in1=sm[:, 16:24].unsqueeze(2).broadcast_to([128, 8, 16]), op=ALU.mult),
                 r=[gn, "sm"], w=[gn])
        def peer_gather(l, Xt, Xn, gp):
            EIDX, GATE, XNB = EIDXs[gp], GATEs[gp], XNBs[gp]
            en, gn, xnbn = f"EIDX{gp}", f"GATE{gp}", f"XNB{gp}"
            gst = peer_gst
            uvr = [f"uvb{l}_{g}" for g in range(NB)]
            pend = []
            for hk in range(128):
                K.mark()
                s = gst["n"] % NB
                gst["n"] += 1
                K.dma("pool", lambda e, s=s, hk=hk: e.indirect_dma_start(
                    out=G[s][:], out_offset=None, in_=uvb_d[l],
                    in_offset=bass.IndirectOffsetOnAxis(ap=EIDX[:, hk:hk + 1], axis=0)),
                    f"g{s}", r=[en] + uvr, w=[f"G{s}"])
                K.op("dve", lambda e, s=s, hk=hk: e.scalar_tensor_tensor(
                    out=G[s][:, 0:1024], in0=G[s][:, 0:1024], scalar=1.0, in1=XNB[:], op0=ALU.mult, op1=ALU.mult,
                    accum_out=APRE[:, hk:hk + 1]), r=[f"G{s}", xnbn], w=[f"G{s}", "APRE"])
                pend.append((hk, s))
                if len(pend) == 2:
                    h0 = pend[0][0]
                    K.op("act", lambda e, h0=h0: e.activation(out=WGT[:, h0:h0 + 2], in_=APRE[:, h0:h0 + 2], func=AF.Gelu),
                         r=["APRE"], w=["WGT"])
                    for (h2, s2) in pend:
                        K.op("act", lambda e, h2=h2: e.activation(out=WGT[:, h2:h2 + 1], in_=WGT[:, h2:h2 + 1], func=AF.Copy,
                                                                 scale=GATE[:, h2:h2 + 1]), r=["WGT", gn], w=["WGT"])
                        d = gst["d"] % 8
                        gst["d"] += 1
                        K.op("act", lambda e, h2=h2, d=d: e.activation(out=DG[d][:], in_=identb[:], func=AF.Copy, scale=WGT[:, h2:h2 + 1]),
                             r=["WGT", "identb"], w=[f"DG{d}"])
                        for hf in range(2):
                            K.op("pe", lambda e, h2=h2, s2=s2, d=d, hf=hf: e.matmul(
                                PS[6 + hf][:, :], lhsT=DG[d][:], rhs=G[s2][:, 1024 + hf * 512:1536 + hf * 512],
                                start=(h2 == 0), stop=(h2 == 127)), r=[f"DG{d}", f"G{s2}"], w=[pb[6 + hf]])
                    pend = []
            K.mark()
            for hf in range(2):
                K.op("dve", lambda e, hf=hf: e.tensor_tensor(out=Xt[:, hf * 512:(hf + 1) * 512], in0=PS[6 + hf][:, :],
                                                            in1=Xt[:, hf * 512:(hf + 1) * 512], op=ALU.add),
                     r=[pb[6 + hf], Xn], w=[Xn])

        peer_gst = {"n": 0, "d": 0}

        def layer0(t, Xt, Xn, par):
            sample = (t == NT)
            first = (t == 0)
            last = (t == NT - 1)
            Uc, Up = U[par], U[1 - par]
            Ucn, Upn = f"U{par}", f"U{1 - par}"
            KTc, KTp = KT[par], KT[1 - par]
            VBc, VBp = VB[par], VB[1 - par]
            kcn, kpn, vcn, vpn = f"KT{par}", f"KT{1 - par}", f"VB{par}", f"VB{1 - par}"
            if first:
                K.op("dve", lambda e: e.memset(Up[:], 0.0), w=[Upn])
                K.op("dve", lambda e: e.memset(KTp[:], 0.0), w=[kpn])
                K.op("dve", lambda e: e.memset(VBp[:], 0.0), w=[vpn])
            if sample:
                K.op("dve", lambda e: e.memset(Up[:], 0.0), w=[Upn])
                K.dma("sp", lambda e: e.dma_start(out=Up[126:128, :], in_=cconv_d), "cc", w=[Upn])
                K.dma("sp", lambda e: e.dma_start(out=CK[:], in_=ck_d), "ck", w=["CK"])
                K.dma("sp", lambda e: e.dma_start(out=CV[:], in_=cv_d), "cv", w=["CV"])
                K.op("pe", lambda e: e.transpose(PS[1][:, 0:128], CK[:], ident), r=["CK", "cst"], w=[pb[1]])
                K.op("act", lambda e: e.activation(out=KTp[:], in_=PS[1][:, 0:128], func=AF.Copy), r=[pb[1]], w=[kpn])
                K.op("act", lambda e: e.activation(out=VBp[:], in_=CV[:], func=AF.Copy), r=["CV"], w=[vpn])
                K.dma("sp", lambda e: e.dma_start(out=ks_d[0:112, :], in_=CK[16:128, :]), "o_ks", r=["CK"], w=["o_ks"])
                K.dma("sp", lambda e: e.dma_start(out=vs_d[0:112, :], in_=CV[16:128, :]), "o_vs", r=["CV"], w=["o_vs"])
            rmsnorm(Xt[:], Xn, None, None, XN[:], "XN")
            transpose8(XN, "XN", XT, "XT", bf=True)
            bg = W[0][:, 0:512]
            cg = W[0][:, 512:1024]
            qk = W[0][:, 1024:1664]

            def cons(ci, c0, wc, b):
                if ci == 0:
                    K.op("act", lambda e: e.activation(out=bg, in_=PS[b][:, :], func=AF.Copy), r=[pb[b]], w=["W0"])
                elif ci == 1:
                    K.op("act", lambda e: e.activation(out=cg, in_=PS[b][:, :], func=AF.Copy), r=[pb[b]], w=["W0"])
                elif ci == 2:
                    K.op("dve", lambda e: e.tensor_tensor(out=Uc[:], in0=PS[b][:, :], in1=cg, op=ALU.mult),
                         r=[pb[b], "W0"], w=[Ucn])
                elif ci == 3:
                    K.op("act", lambda e: e.activation(out=qk[:, 0:512], in_=PS[b][:, :], func=AF.Copy), r=[pb[b]], w=["W0"])
                else:
                    K.op("act", lambda e: e.activation(out=qk[:, 512:640], in_=PS[b][:, 0:128], func=AF.Copy), r=[pb[b]], w=["W0"])
                    K.op("act", lambda e: e.activation(out=VF[:], in_=PS[b][:, 128:256], func=AF.Copy), r=[pb[b]], w=["VF"])
                    K.op("act", lambda e: e.activation(out=VBc[:], in_=PS[b][:, 128:256], func=AF.Copy), r=[pb[b]], w=[vcn])

            dense(XT, "XT", "w_in", cons)
            for (bank, cc, cp) in ((0, C_S1C, C_S1P), (1, C_S2C, C_S2P)):
                K.op("pe", lambda e, bank=bank, cp=cp: e.matmul(PS[bank][:, :], lhsT=cst[:, cp:cp + 128], rhs=Up[:], start=True, stop=False),
                     r=["cst", Upn], w=[pb[bank]])
                K.op("pe", lambda e, bank=bank, cc=cc: e.matmul(PS[bank][:, :], lhsT=cst[:, cc:cc + 128], rhs=Uc[:], start=False, stop=True),
                     r=["cst", Ucn], w=[pb[bank]])
            t0 = W[1][:, 1024:1536]
            t1 = W[1][:, 1536:2048]
            mixin = MIX[:, :]
            K.op("dve", lambda e: e.tensor_tensor(out=t0, in0=Uc[:], in1=convw[:, 2, :], op=ALU.mult), r=[Ucn, "convw"], w=["W1"])
            K.op("dve", lambda e: e.tensor_tensor(out=t1, in0=PS[0][:, :], in1=convw[:, 1, :], op=ALU.mult), r=[pb[0], "convw"], w=["W1"])
            K.op("dve", lambda e: e.tensor_tensor(out=t0, in0=t0, in1=t1, op=ALU.add), r=["W1"], w=["W1"])
            K.op("dve", lambda e: e.tensor_tensor(out=t1, in0=PS[1][:, :], in1=convw[:, 0, :], op=ALU.mult), r=[pb[1], "convw"], w=["W1"])
            K.op("dve", lambda e: e.tensor_tensor(out=t0, in0=t0, in1=t1, op=ALU.add), r=["W1"], w=["W1"])
            K.op("dve", lambda e: e.tensor_tensor(out=mixin[:, 0:512], in0=t0, in1=bg, op=ALU.mult), r=["W1", "W0"], w=["MIX"])
            sq = W[1][:, 0:640]
            K.op("act", lambda e: e.activation(out=sq, in_=qk, func=AF.Square), r=["W0"], w=["W1"])
            K.op("dve", lambda e: e.tensor_reduce(out=sm[:, 32:42], in_=sq.rearrange("p (h d) -> p h d", d=64), axis=AX.X, op=ALU.add),
                 r=["W1"], w=["sm"])
            K.op("act", lambda e: e.activation(out=sm[:, 48:58], in_=sm[:, 32:42], func=AF.Sqrt, scale=1.0 / 64, bias=EPS), r=["sm"], w=["sm"])
            K.op("dve", lambda e: e.reciprocal(out=sm[:, 64:74], in_=sm[:, 48:58]), r=["sm"], w=["sm"])
            q3 = QKN[:, :].rearrange("p (h d) -> p h d", d=64)
            K.op("dve", lambda e: e.tensor_tensor(out=q3, in0=qk.rearrange("p (h d) -> p h d", d=64),
                                                  in1=sm[:, 64:74].unsqueeze(2).broadcast_to([128, 10, 64]), op=ALU.mult),
                 r=["W0", "sm"], w=["QKN"])
            K.op("dve", lambda e: e.tensor_tensor(out=QKN[:], in0=QKN[:], in1=qkg[:], op=ALU.mult), r=["QKN", "qkg"], w=["QKN"])
            for i in range(4):
                K.op("pe", lambda e, i=i: e.transpose(PS[0][:, i * 128:(i + 1) * 128], QKN[:, i * 128:(i + 1) * 128], ident),
                     r=["QKN", "cst"], w=[pb[0]])
            K.op("pe", lambda e: e.transpose(PS[1][:, 0:128], QKN[:, 512:640], ident), r=["QKN", "cst"], w=[pb[1]])
            QT = H[0]
            K.op("act", lambda e: e.activation(out=QT[:, 0:512], in_=PS[0][:, :], func=AF.Copy), r=[pb[0]], w=["H0"])
            K.op("act", lambda e: e.activation(out=KTc[:], in_=PS[1][:, 0:128], func=AF.Copy), r=[pb[1]], w=[kcn])
            def sloc(j):
                r_, i_ = j % 2, j // 2
                return 2 + r_ * 2 + i_ // 2, (i_ % 2) * 256
            for j in range(8):
                b, co = sloc(j)
                r_ = j % 2
                lq = QT[r_ * 64:(r_ + 1) * 64, (j // 2) * 128:(j // 2 + 1) * 128]
                K.op("pe", lambda e, b=b, co=co, lq=lq, r_=r_: e.matmul(PS[b][:, co:co + 128], lhsT=lq, rhs=KTp[r_ * 64:(r_ + 1) * 64, :],
                                                                     start=True, stop=True), r=["H0", kpn], w=[pb[b]])
                K.op("pe", lambda e, b=b, co=co, lq=lq, r_=r_: e.matmul(PS[b][:, co + 128:co + 256], lhsT=lq, rhs=KTc[r_ * 64:(r_ + 1) * 64, :],
                                                                     start=True, stop=True), r=["H0", kcn], w=[pb[b]])
            mi = 2 if sample else (0 if first else 1)
            Ssb = W[2]
            for b in range(2, 6):
                heads = [j for j in range(8) if sloc(j)[0] == b]
                heads.sort(key=lambda j: sloc(j)[1])
                sl = Ssb[:, (b - 2) * 512:(b - 1) * 512]
                K.op("dve", lambda e, b=b, sl=sl: e.tensor_tensor(out=sl, in0=PS[b][:, :], in1=msk[:, mi, :], op=ALU.add),
                     r=[pb[b], "msk"], w=["W2"])
                mx = sm[:, 80 + 2 * (b - 2):82 + 2 * (b - 2)]
                K.op("dve", lambda e, sl=sl, mx=mx: e.tensor_reduce(out=mx, in_=sl.rearrange("p (h k) -> p h k", h=2), axis=AX.X, op=ALU.max),
                     r=["W2"], w=["sm"])
                for hh, j in enumerate(heads):
                    m1 = sm[:, 80 + 2 * (b - 2) + hh:81 + 2 * (b - 2) + hh]
                    nm = sm[:, 96 + j:97 + j]
                    K.op("dve", lambda e, m1=m1, nm=nm, j=j: e.tensor_scalar(out=nm, in0=m1, scalar1=sinkr[:, j:j + 1], scalar2=-1.0,
                                                                            op0=ALU.max, op1=ALU.mult), r=["sm", "sinkr"], w=["sm"])
                    K.op("act", lambda e, sl=sl, hh=hh, nm=nm, j=j: e.activation(
                        out=PB[:, j, :], in_=sl[:, hh * 256:(hh + 1) * 256], func=AF.Exp, bias=nm, scale=1.0,
                        accum_out=sm[:, 112 + j:113 + j]), r=["W2", "sm"], w=["PB", "sm"])
            K.op("dve", lambda e: e.tensor_tensor(out=sm[:, 128:136], in0=sinkr[:], in1=sm[:, 96:104], op=ALU.add), r=["sm", "sinkr"], w=["sm"])
            K.op("act", lambda e: e.activation(out=sm[:, 128:136], in_=sm[:, 128:136], func=AF.Exp), r=["sm"], w=["sm"])
            K.op("dve", lambda e: e.tensor_tensor(out=sm[:, 128:136], in0=sm[:, 128:136], in1=sm[:, 112:120], op=ALU.add), r=["sm"], w=["sm"])
            K.op("dve", lambda e: e.reciprocal(out=sm[:, 136:144], in_=sm[:, 128:136]), r=["sm"], w=["sm"])
            for j in range(8):
                for kb in range(2):
                    n = j * 2 + kb
                    bank = n // 8
                    pv = PS[bank][:, :].bitcast(BF)
                    K.op("pe", lambda e, pv=pv, n=n, j=j, kb=kb: e.transpose(
                        pv[:, (n % 8) * 128:(n % 8 + 1) * 128], PB[:, j, kb * 128:(kb + 1) * 128], identb[:]),
                        r=["PB", "identb"], w=[pb[bank]])
            for bank in (0, 1):
                pv = PS[bank][:, :].bitcast(BF)
                K.op("act", lambda e, pv=pv, bank=bank: e.activation(
                    out=PT[:, bank * 8:(bank + 1) * 8, :], in_=pv.rearrange("p (a n) -> p a n", a=8), func=AF.Copy),
                    r=[pb[bank]], w=["PT"])
            for j in range(8):
                g = j % 2
                K.op("pe", lambda e, j=j, g=g: e.matmul(PS[2][:, j * 64:(j + 1) * 64], lhsT=PT[:, 2 * j, :], rhs=VBp[:, g * 64:(g + 1) * 64],
                                                     start=True, stop=False), r=["PT", vpn], w=[pb[2]])
                K.op("pe", lambda e, j=j, g=g: e.matmul(PS[2][:, j * 64:(j + 1) * 64], lhsT=PT[:, 2 * j + 1, :], rhs=VBc[:, g * 64:(g + 1) * 64],
                                                     start=False, stop=True), r=["PT", vcn], w=[pb[2]])
            K.op("dve", lambda e: e.tensor_tensor(
                out=mixin[:, 512:1024].rearrange("p (h d) -> p h d", d=64), in0=PS[2][:, :].rearrange("p (h d) -> p h d", d=64),
                in1=sm[:, 136:144].unsqueeze(2).broadcast_to([128, 8, 64]), op=ALU.mult), r=[pb[2], "sm"], w=["MIX"])
            if last or sample:
                if sample:
                    K.dma("sp", lambda e: e.dma_start(out=convs_d, in_=Uc[14:16, :]), "o_cs", r=[Ucn], w=["o_cs"])
                    K.dma("sp", lambda e: e.dma_start(out=ks_d[112:128, :], in_=QKN[0:16, 512:640]), "o_ks2", r=["QKN"], w=["o_ks2"])
                    K.dma("sp", lambda e: e.dma_start(out=vs_d[112:128, :], in_=VF[0:16, :]), "o_vs2", r=["VF"], w=["o_vs2"])
                else:
                    K.dma("sp", lambda e: e.dma_start(out=convp_d, in_=Uc[126:128, :]), "o_cp", r=[Ucn], w=["o_cp"])
                    K.dma("sp", lambda e: e.dma_start(out=kp_d, in_=QKN[:, 512:640]), "o_kp", r=["QKN"], w=["o_kp"])
                    K.dma("sp", lambda e: e.dma_start(out=vp_d, in_=VF[:]), "o_vp", r=["VF"], w=["o_vp"])
            transpose8(MIX, "MIX", XT, "XT")

            def cons2(ci, c0, wc, b):
                K.op("dve", lambda e: e.tensor_tensor(out=Xt[:, c0:c0 + wc], in0=PS[b][:, :], in1=Xt[:, c0:c0 + wc], op=ALU.add),
                     r=[pb[b], Xn], w=[Xn])
            dense(XT, "XT", "w_out0", cons2)

        def layer1(t, Xt, Xn, par):
            sample = (t == NT)
            first = (t == 0)
            last = (t == NT - 1)
            if first:
                K.op("dve", lambda e: e.memset(S32[:], 0.0), w=["S32"])
                K.op("dve", lambda e: e.memset(SBF[par][0][:], 0.0), w=[f"SBF{par}0"])
            if sample:
                K.dma("sp", lambda e: e.dma_start(out=S32[:], in_=shg_d.rearrange("h d e -> d h e")), "shg", w=["S32"])
                K.op("act", lambda e: e.activation(out=SBF[par][0][:], in_=S32[:], func=AF.Copy), r=["S32"], w=[f"SBF{par}0"])
            rmsnorm(Xt[:], Xn, None, None, XN[:], "XN")
            transpose8(XN, "XN", XT, "XT", bf=True)
            qs = W[0][:, 0:1024]
            lf = W[0][:, 1024:2048]
            kk = W[1][:, 0:1024]
            sg = W[1][:, 1024:2048]
            tmp = W[2][:, 0:1024]
            vv = H[0]

            def cons(ci, c0, wc, b):
                h = (ci % 2) * 512
                if ci < 2:
                    K.op("act", lambda e: e.activation(out=qs[:, h:h + 512], in_=PS[b][:, :], func=AF.Copy), r=[pb[b]], w=["W0"])
                elif ci < 4:
                    K.op("act", lambda e: e.activation(out=tmp[:, h:h + 512], in_=PS[b][:, :], func=AF.Sigmoid, scale=-1.0), r=[pb[b]], w=["W2"])
                    K.op("dve", lambda e: e.tensor_tensor(out=kk[:, h:h + 512], in0=tmp[:, h:h + 512], in1=omlr[:, h:h + 512], op=ALU.mult),
                         r=["W2", "omlr"], w=["W1"])
                    K.op("act", lambda e: e.activation(out=lf[:, h:h + 512], in_=kk[:, h:h + 512], func=AF.Ln, scale=-1.0, bias=1.0), r=["W1"], w=["W0"])
                    if sample:
                        K.op("dve", lambda e: e.tensor_scalar(out=lf[:, h:h + 512], in0=lf[:, h:h + 512], scalar1=validc, scalar2=None, op0=ALU.mult),
                             r=["W0", "cst"], w=["W0"])
                        K.op("dve", lambda e: e.tensor_scalar(out=kk[:, h:h + 512], in0=kk[:, h:h + 512], scalar1=validc, scalar2=None, op0=ALU.mult),
                             r=["W1", "cst"], w=["W1"])
                elif ci < 6:
                    K.op("act", lambda e: e.activation(out=vv[:, h:h + 512], in_=PS[b][:, :], func=AF.Copy), r=[pb[b]], w=["H0"])
                else:
                    K.op("act", lambda e: e.activation(out=sg[:, h:h + 512], in_=PS[b][:, :], func=AF.Silu), r=[pb[b]], w=["W1"])

            dense(XT, "XT", "hw_in", cons)
            for hf in range(2):
                K.op("pe", lambda e, hf=hf: e.matmul(PS[4 + hf][:, :], lhsT=tri, rhs=lf[:, hf * 512:(hf + 1) * 512], start=True, stop=True),
                     r=["cst", "W0"], w=[pb[4 + hf]])
                K.op("pe", lambda e, hf=hf: e.matmul(PS[hf][:, :], lhsT=rev, rhs=lf[:, hf * 512:(hf + 1) * 512], start=True, stop=True),
                     r=["cst", "W0"], w=[pb[hf]])
            for h in range(8):
                K.op("pe", lambda e, h=h: e.matmul(PS[2][:, 2 * h:2 * h + 2], lhsT=lf[:, h * 128:(h + 1) * 128], rhs=blkind, start=True, stop=True),
                     r=["cst", "W0"], w=[pb[2]])
            K.op("act", lambda e: e.activation(out=DEC[:], in_=PS[2][:, 0:16], func=AF.Exp), r=[pb[2]], w=["DEC"])
            qt, kt, kd = H[1], H[2], H[3]
            ex = W[2][:, 1024:2048]
            for hf in range(2):
                K.op("act", lambda e, hf=hf: e.activation(out=ex[:, hf * 512:(hf + 1) * 512], in_=PS[4 + hf][:, :], func=AF.Exp), r=[pb[4 + hf]], w=["W2x"])
            K.op("dve", lambda e: e.tensor_tensor(out=qt[:], in0=qs, in1=ex, op=ALU.mult), r=["W0", "W2x"], w=["H1"])
            for hf in range(2):
                K.op("act", lambda e, hf=hf: e.activation(out=ex[:, hf * 512:(hf + 1) * 512], in_=PS[4 + hf][:, :], func=AF.Exp, scale=-1.0),
                     r=[pb[4 + hf]], w=["W2x"])
            K.op("dve", lambda e: e.tensor_tensor(out=kt[:], in0=kk, in1=ex, op=ALU.mult), r=["W1", "W2x"], w=["H2"])
            for hf in range(2):
                K.op("act", lambda e, hf=hf: e.activation(out=ex[:, hf * 512:(hf + 1) * 512], in_=PS[hf][:, :], func=AF.Exp), r=[pb[hf]], w=["W2x"])
            K.op("dve", lambda e: e.tensor_tensor(out=kd[:], in0=kk, in1=ex, op=ALU.mult), r=["W1", "W2x"], w=["H3"])
            PBf = PB[:, :, :].rearrange("p a b -> p (a b)")
            for (src, sn, bank, off) in ((qt, "H1", 3, 0), (kt, "H2", 4, 1024)):
                pv = PS[bank][:, :].bitcast(BF)
                for h in range(8):
                    K.op("pe", lambda e, pv=pv, src=src, h=h: e.transpose(pv[:, h * 128:(h + 1) * 128], src[:, h * 128:(h + 1) * 128], identb[:]),
                         r=[sn, "identb"], w=[pb[bank]])
                K.op("act", lambda e, pv=pv, off=off: e.activation(out=PBf[:, off:off + 1024], in_=pv, func=AF.Copy), r=[pb[bank]], w=["PB"])
            qtT = PBf[:, 0:1024]
            ktT = PBf[:, 1024:2048]
            for h in range(8):
                for blk in range(2):
                    n = h * 2 + blk
                    bank = (n // 4) % 2
                    co = (n % 4) * 128
                    K.op("pe", lambda e, bank=bank, co=co, h=h, blk=blk: e.matmul(
                        PS[bank][:, co:co + 128], lhsT=kd[blk * 64:(blk + 1) * 64, h * 128:(h + 1) * 128],
                        rhs=vv[blk * 64:(blk + 1) * 64, h * 128:(h + 1) * 128], start=True, stop=True),
                        r=["H3", "H0"], w=[pb[bank]])
                    K.op("dve", lambda e, bank=bank, co=co, h=h, blk=blk: e.scalar_tensor_tensor(
                        out=S32[:, h, :], in0=S32[:, h, :], scalar=DEC[:, 2 * h + blk:2 * h + blk + 1], in1=PS[bank][:, co:co + 128],
                        op0=ALU.mult, op1=ALU.add), r=["S32", "DEC", pb[bank]], w=["S32"])
                    dstb = SBF[par][1] if blk == 0 else SBF[1 - par][0]
                    dn = f"SBF{par}1" if blk == 0 else f"SBF{1 - par}0"
                    K.op("act", lambda e, dstb=dstb, h=h: e.activation(out=dstb[:, h, :], in_=S32[:, h, :], func=AF.Copy), r=["S32"], w=[dn])
            for h in range(8):
                bank = (2, 5)[h // 4]
                K.op("pe", lambda e, bank=bank, h=h: e.matmul(PS[bank][:, (h % 4) * 128:(h % 4 + 1) * 128], lhsT=ktT[:, h * 128:(h + 1) * 128],
                                                           rhs=qtT[:, h * 128:(h + 1) * 128], start=True, stop=True), r=["PB"], w=[pb[bank]])
            for g in range(2):
                K.op("dve", lambda e, g=g: e.tensor_tensor(out=PT[:, 4 * g:4 * g + 4, :].rearrange("p a b -> p (a b)"), in0=PS[(2, 5)[g]][:, :],
                                                          in1=msk[:, 3, :], op=ALU.mult), r=[pb[(2, 5)[g]], "msk"], w=["PT"])
            for h in range(8):
                bank = 3 + h // 4
                co = (h % 4) * 128
                K.op("pe", lambda e, bank=bank, co=co, h=h: e.matmul(PS[bank][:, co:co + 128], lhsT=PT[:, h, :], rhs=vv[:, h * 128:(h + 1) * 128],
                                                                  start=True, stop=False), r=["PT", "H0"], w=[pb[bank]])
                for blk in range(2):
                    K.op("pe", lambda e, bank=bank, co=co, h=h, blk=blk: e.matmul(
                        PS[bank][blk * 64:(blk + 1) * 64, co:co + 128], lhsT=qtT[:, h * 128 + blk * 64:h * 128 + (blk + 1) * 64],
                        rhs=SBF[par][blk][:, h, :], start=False, stop=(blk == 1)),
                        r=["PB", f"SBF{par}{blk}"], w=[pb[bank]])
            osb = W[2][:, 0:1024]
            for g in range(2):
                K.op("act", lambda e, g=g: e.activation(out=osb[:, g * 512:(g + 1) * 512], in_=PS[3 + g][:, :], func=AF.Copy), r=[pb[3 + g]], w=["W2"])
            hm = MIX[:, :]
            rmsnorm(osb, "W2", sg, "W1", hm, "MIX")
            if last or sample:
                od = ss_d if sample else sp_d
                K.dma("sp", lambda e: e.dma_start(out=od.rearrange("h d e -> d h e"), in_=S32[:]), "o_s" + str(int(sample)), r=["S32"], w=["o_s" + str(int(sample))])
            transpose8(hm, "MIX", XT, "XT")

            def cons2(ci, c0, wc, b):
                K.op("dve", lambda e: e.tensor_tensor(out=Xt[:, c0:c0 + wc], in0=PS[b][:, :], in1=Xt[:, c0:c0 + wc], op=ALU.add),
                     r=[pb[b], Xn], w=[Xn])
            dense(XT, "XT", "hw_out", cons2)

        tiles = list(range(NT + 1))

        def Xof(t):
            return X[t % 3], f"X{t % 3}"

        def L0(t, gp):
            Xt, Xn = Xof(t)
            if t < NT:
                K.dma("sp", lambda e: e.dma_start(out=Xt[:], in_=xp_d[t * 128:(t + 1) * 128, :]), f"x{t % 3}", w=[Xn])
            else:
                K.op("dve", lambda e: e.memset(Xt[:], 0.0), w=[Xn])
                K.dma("sp", lambda e: e.dma_start(out=Xt[0:16, :], in_=xs_d), f"x{t % 3}", w=[Xn])
            layer0(t, Xt, Xn, t % 2)
            peer_topk(0, Xt, Xn, gp)

        def L1(t, gp):
            Xt, Xn = Xof(t)
            layer1(t, Xt, Xn, t % 2)
            peer_topk(1, Xt, Xn, gp)

        def G0(t, gp):
            Xt, Xn = Xof(t)
            peer_gather(0, Xt, Xn, gp)

        def G1(t, gp):
            Xt, Xn = Xof(t)
            peer_gather(1, Xt, Xn, gp)
            if t < NT:
                K.dma("sp", lambda e: e.dma_start(out=yp_d[t * 128:(t + 1) * 128, :], in_=Xt[:]), f"y{t % 3}", r=[Xn], w=[f"oy{t % 3}"])
            else:
                K.dma("sp", lambda e: e.dma_start(out=ys_d, in_=Xt[0:16, :]), f"y{t % 3}", r=[Xn], w=[f"oy{t % 3}"])

        Lchain, Gchain = [], []
        i = 0
        while i < len(tiles):
            pair = tiles[i:i + 2]
            i += 2
            for t in pair:
                Lchain.append((L0, t))
            for t in pair:
                Lchain.append((L1, t))
            for t in pair:
                Gchain.append((G0, t))
            for t in pair:
                Gchain.append((G1, t))
        for (f_, t) in Lchain:
            names = ("w_in", "w_out0", "wq0") if f_ is L0 else ("hw_in", "hw_out", "wq1")
            for name in names:
                wseq.extend([(name, c) for c in ctab[name]])

        def record(f_, t, gp):
            K.rec = []
            f_(t, gp)
            r_ = K.rec
            K.rec = None
            return r_

        def emit_alone(recs):
            for r_ in recs:
                K.process(r_)

        emit_alone(record(Lchain[0][0], Lchain[0][1], 0))
        li = 1
        odd_tail = (len(tiles) % 2 == 1)
        for k in range(len(Gchain)):
            gf, gt = Gchain[k]
            grec = record(gf, gt, k % 2)
            import os
            if os.environ.get('PIPE', '1') == '1' and li < len(Lchain) and not (Lchain[li][0] is L1 and gf is G0 and Lchain[li][1] == gt) and not (odd_tail and Lchain[li][1] == tiles[-1]):
                lrec = record(Lchain[li][0], Lchain[li][1], li % 2)
                li += 1
                K.merge(grec, lrec)
            else:
                emit_alone(grec)
            while li <= k + 1 and li < len(Lchain):
                emit_alone(record(Lchain[li][0], Lchain[li][1], li % 2))
                li += 1
        K.wait_all("sp")
        K.emit()
        print("merge_stats", getattr(K, "merge_stats", None))
        print("n_ops", K.n_ops, {e: len(v) for e, v in K.ops.items()})
    return nc


def host_consts():
    c = np.zeros((128, C_N), np.float32)
    s = np.arange(128)[:, None]
    t = np.arange(128)[None, :]
    same = (s // 64) == (t // 64)
    c[:, C_ID:C_ID + 128] = np.eye(128)
    c[:, C_TRI:C_TRI + 128] = (same & (s <= t))
    c[:, C_REV:C_REV + 128] = (same & (s > t))
    c[:, C_S1C:C_S1C + 128] = (s == t - 1)
    c[:, C_S1P:C_S1P + 128] = (s == t - 1 + 128)
    c[:, C_S2C:C_S2C + 128] = (s == t - 2)
    c[:, C_S2P:C_S2P + 128] = (s == t - 2 + 128)
    c[:, C_BLK] = (np.arange(128) < 64)
    c[:, C_BLK + 1] = (np.arange(128) >= 64)
    c[:, C_IOTA:C_IOTA + 16] = np.arange(16)[None, :]
    c[:, C_VALID] = (np.arange(128) < 16)
    c[:, C_IOTA16:C_IOTA16 + 16] = 16.0 * np.arange(16)[None, :]
    m = np.zeros((128, 4, 2, 256), np.float32)
    a = np.zeros((128, 256), np.float32)
    a[0:64, 192:256] = NEG
    a[64:128, 0:64] = NEG
    a0 = a.copy()
    a0[:, 0:128] = NEG
    ms = np.zeros((128, 256), np.float32)
    ms[:, 144:256] = NEG
    m[:, 0] = a0[:, None, :]
    m[:, 1] = a[:, None, :]
    m[:, 2] = ms[:, None, :]
    m = m.reshape(128, 4, 512)
    m[:, 3] = np.tile(c[:, C_TRI:C_TRI + 128], (1, 4))
    return c, m


def host_prep(inp, NT):
    f = lambda a: np.ascontiguousarray(np.asarray(a, dtype=np.float32))
    w_in = f(inp["even_w_in"][0]).copy()
    qcols = w_in[:, 1536:2048].reshape(1024, 8, 64)[:, QPERM, :].reshape(1024, 512)
    w_in[:, 1536:2048] = qcols
    w_out0 = f(inp["even_w_out"][0]).copy()
    w_out0[512:1024] = w_out0[512:1024].reshape(8, 64, 1024)[QPERM].reshape(512, 1024)
    sinks = f(inp["even_sinks"][0])[QPERM]
    keysT = np.ascontiguousarray(f(inp["peer_sub_keys"]).transpose(0, 4, 1, 2, 3).reshape(2, 128, 2048))
    uv = np.concatenate([f(inp["peer_u"]), f(inp["peer_v"])], axis=2)
    cst, msk = host_consts()
    rep = lambda v: np.ascontiguousarray(np.broadcast_to(v, (128,) + v.shape))
    qkg = np.concatenate([np.tile(f(inp["even_q_gain"][0]), 8), np.tile(f(inp["even_k_gain"][0]), 2)])
    gcol = np.concatenate([f(inp["norm_mix"][0]).reshape(8, 128).T, f(inp["norm_mix"][1]).reshape(8, 128).T,
                           f(inp["hgrn_out_gain"][0]).reshape(8, 128).T], axis=1)
    shared = {
        "w_in": w_in, "w_out0": w_out0, "hw_in": f(inp["hgrn_w_in"][0]), "hw_out": f(inp["hgrn_w_out"][0]),
        "wq": f(inp["peer_w_query"]), "keysT": keysT, "uv0": uv[0], "uv1": uv[1], "cst": cst, "msk": msk.astype(ml_dtypes.bfloat16),
        "nffn": rep(f(inp["norm_ffn"])).astype(ml_dtypes.bfloat16), "hlb": rep(f(inp["hgrn_lb"])), "convw": rep(f(inp["even_conv_w"][0])),
        "qkg": rep(qkg), "sinkr": rep(sinks), "gcol": np.ascontiguousarray(gcol),
    }
    maps = []
    for c in range(NCORES):
        d = dict(shared)
        d["xp"] = f(inp["x_prompt"][c][:NT * 128])
        d["xs"] = f(inp["x_sample"][c])
        d["cconv"] = f(inp["cache_conv"][0, c])
        d["ck"] = f(inp["cache_k"][0, c]).reshape(128, 128)
        d["cv"] = f(inp["cache_v"][0, c]).reshape(128, 128)
        d["shg"] = f(inp["state_hgrn"][0, c])
        maps.append(d)
    return maps


def gather_out(res, NT):
    R = res.results
    st = lambda k: np.stack([np.asarray(r[k], dtype=np.float32) for r in R])
    y_p = st("y_p")
    y_s = st("y_s")
    return (y_p, y_s,
            st("conv_p")[None], st("k_p").reshape(1, NCORES, 128, 2, 64), st("v_p").reshape(1, NCORES, 128, 2, 64), st("s_p")[None],
            st("conv_s")[None], st("k_s").reshape(1, NCORES, 128, 2, 64), st("v_s").reshape(1, NCORES, 128, 2, 64), st("s_s")[None])


def run(inp, NT, stage=99, trace=False):
    nc = build(NT, stage)
    maps = host_prep(inp, NT)
    res = run_bass_kernel_spmd(nc, maps, core_ids=list(range(NCORES)), trace=trace)
    return gather_out(res, NT), res


def kernel(**inputs):
    out, _ = run(inputs, 32)
    return out
```

```python
import numpy as np
import ml_dtypes
from contextlib import ExitStack
import concourse.bass as bass
import concourse.mybir as mybir
from concourse.bass_utils import run_bass_kernel_spmd

F32 = mybir.dt.float32
BF = mybir.dt.bfloat16
U32 = mybir.dt.uint32
ALU = mybir.AluOpType
AF = mybir.ActivationFunctionType
AX = mybir.AxisListType

NCORES = 8
NE = 16384
NB = 13
NEG = -30000.0
EPS = 1e-6
QPERM = [0, 4, 1, 5, 2, 6, 3, 7]

C_ID, C_TRI, C_REV, C_S1C, C_S1P, C_S2C, C_S2P = 0, 128, 256, 384, 512, 640, 768
C_BLK, C_IOTA, C_VALID, C_IOTA16, C_N = 896, 898, 914, 916, 932


class Buf:
    __slots__ = ("name", "last_w", "readers")

    def __init__(self, name):
        self.name = name
        self.last_w = None
        self.readers = {}


class Sched:
    ENG = ("pe", "act", "dve", "pool", "sp")

    def __init__(self, nc, stack):
        self.nc = nc
        self.stack = stack
        self.sems = {}
        self.cnt = {}
        self.ops = {e: [] for e in self.ENG}
        self.waited = {e: {} for e in self.ENG}
        self.bufs = {}
        for e in self.ENG:
            self._mksem("E_" + e)
        self.n_ops = 0
        self.rec = None

    def _mksem(self, key):
        self.sems[key] = self.stack.enter_context(self.nc.semaphore(key))
        self.cnt[key] = 0

    def B(self, name):
        b = self.bufs.get(name)
        if b is None:
            b = self.bufs[name] = Buf(name)
        return b

    def _collect(self, eng, reads, writes):
        need = {}

        def add(tok):
            if tok is None:
                return
            k, v = tok
            if eng == "pe" and k == "E_pe":
                return
            if need.get(k, 0) < v:
                need[k] = v

        for b in reads:
            add(b.last_w)
        for b in writes:
            add(b.last_w)
            for k, v in b.readers.items():
                add((k, v))
        waits = []
        wd = self.waited[eng]
        for k, v in need.items():
            if wd.get(k, 0) < v:
                wd[k] = v
                waits.append((k, v))
        return waits

    def _commit(self, tok, reads, writes):
        k, v = tok
        for b in reads:
            if b.readers.get(k, 0) < v:
                b.readers[k] = v
        for b in writes:
            b.last_w = tok
            b.readers = {}

    def op(self, eng, fn, r=(), w=()):
        rec = ("op", eng, fn, tuple(r), tuple(w), None)
        if self.rec is not None:
            self.rec.append(rec)
        else:
            self.process(rec)

    def dma(self, eng, fn, semkey, r=(), w=()):
        rec = ("dma", eng, fn, tuple(r), tuple(w), semkey)
        if self.rec is not None:
            self.rec.append(rec)
        else:
            self.process(rec)

    def mark(self):
        if self.rec is not None:
            self.rec.append(("mark",))

    def process(self, rec):
        if rec[0] == "mark":
            return
        kind, eng, fn, r, w, semkey = rec
        reads = [self.B(x) for x in r]
        writes = [self.B(x) for x in w]
        waits = self._collect(eng, reads, writes)
        if kind == "op":
            key, inc = "E_" + eng, 1
        else:
            key, inc = "D_" + semkey, 16
            if key not in self.sems:
                self._mksem(key)
        self.cnt[key] += inc
        tok = (key, self.cnt[key])
        self.ops[eng].append((waits, fn, key, inc))
        self._commit(tok, reads, writes)
        self.n_ops += 1

    def merge(self, grec, lrec, D=2):
        groups, cur = [], []
        for r_ in grec:
            if r_[0] == "mark":
                if cur:
                    groups.append(cur)
                cur = []
            else:
                cur.append(r_)
        if cur:
            groups.append(cur)
        n = max(1, len(groups))
        cap = max(4, -(-len(lrec) // max(1, int(0.7 * n))))
        last_w, last_r = {}, {}
        assign = []
        g, cnt = 0, 0
        for rec in lrec:
            if rec[0] == "mark":
                assign.append(g)
                continue
            kind, eng, fn, r, w, _ = rec
            req = g
            if kind == "op" and eng == "dve":
                for b in r + w:
                    lw = last_w.get(b)
                    if lw is not None and lw[0] != "dve":
                        req = max(req, lw[1] + (D if lw[0] != "dma" else D + 1))
                for b in w:
                    lr = last_r.get(b)
                    if lr is not None and lr[0] != "dve":
                        req = max(req, lr[1] + D)
            if req > g:
                g, cnt = req, 0
            if cnt >= cap:
                g, cnt = g + 1, 0
            assign.append(g)
            cnt += 1
            pe_ = eng if kind == "op" else "dma"
            for b in r:
                last_r[b] = (pe_, g)
            for b in w:
                last_w[b] = (pe_, g)
                last_r.pop(b, None)
        li = 0
        gi = 0
        while gi < len(groups) or li < len(lrec):
            if gi < len(groups):
                for r_ in groups[gi]:
                    self.process(r_)
            while li < len(lrec) and (assign[li] <= gi or gi >= len(groups)):
                self.process(lrec[li])
                li += 1
            gi += 1
        self.merge_stats = (len(groups), (assign[-1] if assign else 0), len(lrec), cap)

    def wait_all(self, eng):
        bl = list(self.bufs.values())
        waits = self._collect(eng, bl, bl)
        self.ops[eng].append((waits, None, None, 0))

    def emit(self):
        nc, sems, ops = self.nc, self.sems, self.ops

        def run(engine, lst):
            for waits, fn, key, inc in lst:
                for k, v in waits:
                    engine.wait_ge(sems[k], v)
                if fn is not None:
                    fn(engine).then_inc(sems[key], inc)

        with nc.Block() as block:
            @block.tensor
            def _(e):
                run(e, ops["pe"])

            @block.scalar
            def _(e):
                run(e, ops["act"])

            @block.vector
            def _(e):
                run(e, ops["dve"])

            @block.gpsimd
            def _(e):
                run(e, ops["pool"])

            @block.sync
            def _(e):
                run(e, ops["sp"])


WSPEC = [("w_in", 2304), ("w_out0", 1024), ("wq0", 2048), ("hw_in", 4096), ("hw_out", 1024), ("wq1", 2048)]


def chunk_table():
    tab, idx = {}, 0
    for name, n in WSPEC:
        lst = []
        c0 = 0
        while c0 < n:
            wc = min(512, n - c0)
            lst.append((idx, c0, wc))
            idx += 1
            c0 += wc
        tab[name] = lst
    return tab, idx


def build(NT, stage=99):
    nc = bass.Bass("TRN2", target_bir_lowering=False)
    T = NT * 128

    def din(name, shape, dt=F32):
        return nc.dram_tensor(name, shape, dt, kind="ExternalInput").ap()

    def dout(name, shape, dt=F32):
        return nc.dram_tensor(name, shape, dt, kind="ExternalOutput").ap()

    xp_d = din("xp", [T, 1024])
    xs_d = din("xs", [16, 1024])
    cconv_d = din("cconv", [2, 512])
    ck_d = din("ck", [128, 128])
    cv_d = din("cv", [128, 128])
    shg_d = din("shg", [8, 128, 128])
    w_d = {
        "w_in": din("w_in", [1024, 2304]),
        "w_out0": din("w_out0", [1024, 1024]),
        "hw_in": din("hw_in", [1024, 4096]),
        "hw_out": din("hw_out", [1024, 1024]),
    }
    wq_d = din("wq", [2, 1024, 2048])
    w_d["wq0"] = wq_d[0]
    w_d["wq1"] = wq_d[1]
    keysT_d = din("keysT", [2, 128, 2048])
    uv_d = [din("uv0", [NE, 2048]), din("uv1", [NE, 2048])]
    cst_d = din("cst", [128, C_N])
    msk_d = din("msk", [128, 4, 512], BF)
    nffn_d = din("nffn", [128, 2, 1024], BF)
    hlb_d = din("hlb", [128, 2, 1024])
    convw_d = din("convw", [128, 3, 512])
    qkg_d = din("qkg", [128, 640])
    sink_d = din("sinkr", [128, 8])
    gcol_d = din("gcol", [128, 24])

    yp_d = dout("y_p", [T, 1024])
    ys_d = dout("y_s", [16, 1024])
    convp_d = dout("conv_p", [2, 512])
    kp_d = dout("k_p", [128, 128])
    vp_d = dout("v_p", [128, 128])
    sp_d = dout("s_p", [8, 128, 128])
    convs_d = dout("conv_s", [2, 512])
    ks_d = dout("k_s", [128, 128])
    vs_d = dout("v_s", [128, 128])
    ss_d = dout("s_s", [8, 128, 128])

    ctab, nchunks = chunk_table()
    wsc_d = nc.dram_tensor("wsc", [nchunks, 128, 4096], BF, kind="Internal").ap()
    uvb_d = [nc.dram_tensor(f"uvb{l}", [NE, 2048], BF, kind="Internal").ap() for l in range(2)]

    with ExitStack() as st:
        K = Sched(nc, st)

        def sb(name, shape, dt=F32):
            return st.enter_context(nc.sbuf_tensor("sb_" + name, shape, dt))

        PS = [st.enter_context(nc.psum_tensor(f"ps{i}", [128, 512], F32)) for i in range(8)]
        pb = [f"ps{i}" for i in range(8)]

        cst = sb("cst", [128, C_N])
        msk = sb("msk", [128, 4, 512], BF)
        identb = sb("identb", [128, 128], BF)
        nffn = sb("nffn", [128, 2, 1024], BF)
        omlr = sb("omlr", [128, 1024])
        convw = sb("convw", [128, 3, 512])
        qkg = sb("qkg", [128, 640])
        sinkr = sb("sinkr", [128, 8])
        gcol = sb("gcol", [128, 24])
        keysT = sb("keysT", [128, 2, 2048], BF)

        X = [sb(f"X{i}", [128, 1024]) for i in range(3)]
        XN = sb("XN", [128, 1024], BF)
        XT = sb("XT", [128, 8, 128], BF)
        WR = [sb(f"WR{i}", [128, 8, 512], BF) for i in range(3)]
        W = [sb(f"W{i}", [128, 2048]) for i in range(3)]
        MIX = sb("MIX", [128, 1024])
        H = [sb(f"H{i}", [128, 1024], BF) for i in range(4)]
        G = [sb(f"G{i}", [128, 2048], BF) for i in range(NB)]
        DG = [sb(f"DG{i}", [128, 128], BF) for i in range(8)]
        XNBs = [sb(f"XNB{i}", [128, 1024], BF) for i in range(2)]
        U = [sb(f"U{i}", [128, 512]) for i in range(2)]
        QKN = sb("QKN", [128, 640])
        VF = sb("VF", [128, 128])
        KT = [sb(f"KT{i}", [128, 128], BF) for i in range(2)]
        VB = [sb(f"VB{i}", [128, 128], BF) for i in range(2)]
        PB = sb("PB", [128, 8, 256], BF)
        PT = sb("PT", [128, 16, 128], BF)
        S32 = sb("S32", [128, 8, 128])
        SBF = [[sb(f"SBF{p}{b}", [128, 8, 128], BF) for b in range(2)] for p in range(2)]
        CK = sb("CK", [128, 128])
        CV = sb("CV", [128, 128])
        sm = sb("sm", [128, 256])
        SV = sb("SV", [128, 256])
        SI = sb("SI", [128, 256], U32)
        SIF = sb("SIF", [128, 256])
        CVv = sb("CVv", [128, 128])
        CI = sb("CI", [128, 128], U32)
        IF_ = sb("IF", [128, 128])
        JF_ = sb("JF", [128, 128])
        E1 = sb("E1", [128, 128])
        E2 = sb("E2", [128, 128])
        EIDXs = [sb(f"EIDX{i}", [128, 128], U32) for i in range(2)]
        GATEs = [sb(f"GATE{i}", [128, 128]) for i in range(2)]
        APRE = sb("APRE", [128, 128])
        WGT = sb("WGT", [128, 128])
        DEC = sb("DEC", [128, 16])

        ident = cst[:, C_ID:C_ID + 128]
        tri = cst[:, C_TRI:C_TRI + 128]
        rev = cst[:, C_REV:C_REV + 128]
        blkind = cst[:, C_BLK:C_BLK + 2]
        iota16 = cst[:, C_IOTA:C_IOTA + 16]
        validc = cst[:, C_VALID:C_VALID + 1]
        iota16x = cst[:, C_IOTA16:C_IOTA16 + 16]

        def ld(dst, src, key, w):
            K.dma("sp", lambda e: e.dma_start(out=dst, in_=src), key, w=[w])

        ld(cst[:], cst_d, "su0", "cst")
        ld(msk[:], msk_d, "su1", "msk")
        ld(nffn[:], nffn_d, "su2", "nffn")
        ld(W[0][:], hlb_d.rearrange("p a b -> p (a b)"), "su3", "W0")
        ld(convw[:], convw_d, "su4", "convw")
        ld(qkg[:], qkg_d, "su5", "qkg")
        ld(sinkr[:], sink_d, "su6", "sinkr")
        ld(gcol[:], gcol_d, "su7", "gcol")
        K.op("act", lambda e: e.activation(out=identb[:], in_=ident, func=AF.Copy), r=["cst"], w=["identb"])
        K.op("dve", lambda e: e.tensor_tensor(out=omlr[:], in0=W[0][:, 0:1024], in1=W[0][:, 1024:2048], op=ALU.subtract),
             r=["W0"], w=["omlr"])
        K.op("act", lambda e: e.activation(out=omlr[:], in_=omlr[:], func=AF.Sigmoid), r=["omlr"], w=["omlr"])
        K.op("dve", lambda e: e.tensor_scalar(out=qkg[:, 0:512], in0=qkg[:, 0:512], scalar1=0.125, scalar2=None, op0=ALU.mult),
             r=["qkg"], w=["qkg"])
        for l in range(2):
            ld(W[1 + l][:], keysT_d[l], f"su8{l}", f"W{1 + l}")
            K.op("act", lambda e, l=l: e.activation(out=keysT[:, l, :], in_=W[1 + l][:], func=AF.Copy),
                 r=[f"W{1 + l}"], w=["keysT"])

        gsel = {"w_in": 0, "hw_in": 1, "hw_out": 2}
        pi = 0
        si_ = 0
        for name, n in WSPEC:
            for (idx, c0, wc) in ctab[name]:
                slot = pi % 3
                for kh in range(2):
                    stg = W[si_ % 2]
                    stn = f"W{si_ % 2}"
                    src = w_d[name][kh * 512:(kh + 1) * 512, c0:c0 + wc].rearrange("(k p) n -> p k n", p=128)
                    dst = stg[:, 0:4 * wc].rearrange("p (k n) -> p k n", k=4)
                    K.dma("sp", lambda e, dst=dst, src=src: e.dma_start(out=dst, in_=src), f"ps{si_ % 2}", w=[stn])
                    if name in gsel:
                        g = gsel[name]
                        for k4 in range(4):
                            k = kh * 4 + k4
                            K.op("act", lambda e, dst=dst, slot=slot, k=k, k4=k4, wc=wc, g=g: e.activation(
                                out=WR[slot][:, k, 0:wc], in_=dst[:, k4, :], func=AF.Copy,
                                scale=gcol[:, g * 8 + k:g * 8 + k + 1]),
                                r=[stn, "gcol"], w=[f"WR{slot}"])
                    else:
                        K.op("act", lambda e, dst=dst, slot=slot, wc=wc, kh=kh: e.activation(
                            out=WR[slot][:, kh * 4:kh * 4 + 4, 0:wc], in_=dst, func=AF.Copy), r=[stn], w=[f"WR{slot}"])
                    si_ += 1
                K.dma("sp", lambda e, slot=slot, idx=idx, wc=wc: e.dma_start(
                    out=wsc_d[idx].rearrange("p (k n) -> p k n", k=8)[:, :, 0:wc], in_=WR[slot][:, :, 0:wc]),
                    f"pw{slot}", r=[f"WR{slot}"], w=[f"wsc{idx}"])
                pi += 1

        nslab = 0
        for l in range(2):
            srcv = uv_d[l].rearrange("(p r) c -> p r c", p=128)
            dstv = uvb_d[l].rearrange("(p r) c -> p r c", p=128)
            for r_ in range(128):
                a = nslab % 3
                g_ = nslab % NB
                K.dma("sp", lambda e, a=a, r_=r_, srcv=srcv: e.dma_start(out=W[a][:], in_=srcv[:, r_, :]), f"tc{a}", w=[f"W{a}"])
                if nslab % 2 == 0:
                    K.op("act", lambda e, a=a, g_=g_: e.activation(out=G[g_][:], in_=W[a][:], func=AF.Copy), r=[f"W{a}"], w=[f"G{g_}"])
                else:
                    K.op("dve", lambda e, a=a, g_=g_: e.tensor_copy(out=G[g_][:], in_=W[a][:]), r=[f"W{a}"], w=[f"G{g_}"])
                K.dma("pool", lambda e, g_=g_, r_=r_, dstv=dstv: e.dma_start(out=dstv[:, r_, :], in_=G[g_][:]), f"ts{g_}", r=[f"G{g_}"], w=[f"uvb{l}_{g_}"])
                nslab += 1
        wseq = []
        wst = {"issued": 0, "used": 0}

        def wnext():
            i = wst["used"]
            while wst["issued"] < min(i + 3, len(wseq)):
                j = wst["issued"]
                idx = wseq[j][1][0]
                s = j % 3
                wcj = wseq[j][1][2]
                K.dma("sp", lambda e, s=s, idx=idx, wcj=wcj: e.dma_start(
                    out=WR[s][:, :, 0:wcj], in_=wsc_d[idx].rearrange("p (k n) -> p k n", k=8)[:, :, 0:wcj]),
                    f"wl{s}", r=[f"wsc{idx}"], w=[f"WR{s}"])
                wst["issued"] += 1
            wst["used"] += 1
            return i % 3, wseq[i][1]

        dbank = {"i": 0}

        def dense(xT, xTn, name, consume):
            for ci in range(len(ctab[name])):
                s, (idx, c0, wc) = wnext()
                b = 2 + dbank["i"] % 4
                dbank["i"] += 1
                for k in range(8):
                    K.op("pe", lambda e, b=b, k=k, s=s, wc=wc: e.matmul(
                        PS[b][:, 0:wc], lhsT=xT[:, k, :], rhs=WR[s][:, k, 0:wc], start=(k == 0), stop=(k == 7)),
                        r=[xTn, f"WR{s}"], w=[pb[b]])
                consume(ci, c0, wc, b)

        def transpose8(src, srcn, dstT, dstn, bf=False):
            if bf:
                pv0 = PS[0][:, :].bitcast(BF)
                for k in range(8):
                    K.op("pe", lambda e, k=k: e.transpose(pv0[:, k * 128:(k + 1) * 128], src[:, k * 128:(k + 1) * 128], identb[:]),
                         r=[srcn, "identb"], w=[pb[0]])
                K.op("act", lambda e: e.activation(out=dstT[:, :, :].rearrange("p k n -> p (k n)"), in_=pv0, func=AF.Copy), r=[pb[0]], w=[dstn])
                return
            for k in range(8):
                b = k // 4
                K.op("pe", lambda e, k=k, b=b: e.transpose(
                    PS[b][:, (k % 4) * 128:(k % 4 + 1) * 128], src[:, k * 128:(k + 1) * 128], ident),
                    r=[srcn, "cst"], w=[pb[b]])
            for b in range(2):
                K.op("act", lambda e, b=b: e.activation(
                    out=dstT[:, 4 * b:4 * b + 4, :], in_=PS[b][:, :].rearrange("p (k n) -> p k n", k=4), func=AF.Copy),
                    r=[pb[b]], w=[dstn])

        def rmsnorm(xt, xtn, gain, gainn, out, outn):
            K.op("act", lambda e: e.activation(out=W[1][:, 0:1024], in_=xt, func=AF.Square, accum_out=sm[:, 0:1]),
                 r=[xtn], w=["W1", "sm"])
            K.op("act", lambda e: e.activation(out=sm[:, 1:2], in_=sm[:, 0:1], func=AF.Sqrt, scale=1.0 / 1024, bias=EPS),
                 r=["sm"], w=["sm"])
            K.op("dve", lambda e: e.reciprocal(out=sm[:, 2:3], in_=sm[:, 1:2]), r=["sm"], w=["sm"])
            if gain is None:
                K.op("dve", lambda e: e.tensor_scalar(out=out, in0=xt, scalar1=sm[:, 2:3], scalar2=None, op0=ALU.mult),
                     r=[xtn, "sm"], w=[outn])
            else:
                K.op("dve", lambda e: e.scalar_tensor_tensor(out=out, in0=xt, scalar=sm[:, 2:3], in1=gain,
                                                            op0=ALU.mult, op1=ALU.mult),
                     r=[xtn, "sm", gainn], w=[outn])

        def peer_topk(l, Xt, Xn, gp):
            EIDX, GATE, XNB = EIDXs[gp], GATEs[gp], XNBs[gp]
            en, gn, xnbn = f"EIDX{gp}", f"GATE{gp}", f"XNB{gp}"
            rmsnorm(Xt[:], Xn, nffn[:, l, :], "nffn", XNB[:], xnbn)
            pv0 = PS[0][:, :].bitcast(BF)
            for k in range(8):
                K.op("pe", lambda e, k=k: e.transpose(pv0[:, k * 128:(k + 1) * 128], XNB[:, k * 128:(k + 1) * 128], identb[:]),
                     r=[xnbn, "identb"], w=[pb[0]])
            K.op("act", lambda e: e.activation(out=XT[:, :, :].rearrange("p k n -> p (k n)"), in_=pv0, func=AF.Copy), r=[pb[0]], w=["XT"])
            QT = H[0:2]
            for ci in range(4):
                s, (idx, c0, wc) = wnext()
                for j in range(4):
                    blk = ci * 4 + j
                    b = 2 + blk // 4
                    for k in range(8):
                        K.op("pe", lambda e, b=b, j=j, k=k, s=s: e.matmul(
                            PS[b][:, j * 128:(j + 1) * 128], lhsT=WR[s][:, k, j * 128:(j + 1) * 128], rhs=XT[:, k, :],
                            start=(k == 0), stop=(k == 7)), r=["XT", f"WR{s}"], w=[pb[b]])
                b = 2 + ci
                hq = H[ci // 2]
                K.op("act", lambda e, b=b, hq=hq, ci=ci: e.activation(
                    out=hq[:, (ci % 2) * 512:(ci % 2 + 1) * 512], in_=PS[b][:, :], func=AF.Copy),
                    r=[pb[b]], w=[f"H{ci // 2}"])
            sbanks = [0, 1, 2, 3]
            for blk in range(16):
                b = sbanks[blk // 4]
                hq = H[blk // 8]
                K.op("pe", lambda e, b=b, blk=blk, hq=hq: e.matmul(
                    PS[b][:, (blk % 4) * 128:(blk % 4 + 1) * 128],
                    lhsT=hq[:, (blk % 8) * 128:(blk % 8 + 1) * 128],
                    rhs=keysT[:, l, blk * 128:(blk + 1) * 128], start=True, stop=True),
                    r=[f"H{blk // 8}", "keysT"], w=[pb[b]])
            for g in range(4):
                b = sbanks[g]
                K.op("act", lambda e, b=b, g=g: e.activation(out=W[0][:, g * 512:(g + 1) * 512], in_=PS[b][:, :], func=AF.Copy),
                     r=[pb[b]], w=["W0"])
            for blk in range(16):
                sc = W[0][:, blk * 128:(blk + 1) * 128]
                for half in range(2):
                    o8 = blk * 16 + half * 8
                    K.op("dve", lambda e, sc=sc, o8=o8: e.max(out=SV[:, o8:o8 + 8], in_=sc), r=["W0"], w=["SV"])
                    K.op("dve", lambda e, sc=sc, o8=o8: e.max_index(out=SI[:, o8:o8 + 8], in_max=SV[:, o8:o8 + 8], in_values=sc),
                         r=["W0", "SV"], w=["SI"])
                    if half == 0:
                        K.op("dve", lambda e, sc=sc, o8=o8: e.match_replace(out=sc, in_to_replace=SV[:, o8:o8 + 8],
                                                                            in_values=sc, imm_value=-1e30),
                             r=["W0", "SV"], w=["W0"])
            K.op("dve", lambda e: e.tensor_copy(out=SIF[:], in_=SI[:]), r=["SI"], w=["SIF"])
            sv4 = SV[:, :].rearrange("p (h q k) -> p h q k", h=8, q=2)
            cand = W[1][:, :].rearrange("p (h i j) -> p h i j", h=8, i=16)
            K.op("dve", lambda e: e.tensor_tensor(
                out=cand, in0=sv4[:, :, 0, :].unsqueeze(3).broadcast_to([128, 8, 16, 16]),
                in1=sv4[:, :, 1, :].unsqueeze(2).broadcast_to([128, 8, 16, 16]), op=ALU.add),
                r=["SV"], w=["W1"])
            for h in range(8):
                cd = W[1][:, h * 256:(h + 1) * 256]
                for half in range(2):
                    o8 = h * 16 + half * 8
                    K.op("dve", lambda e, cd=cd, o8=o8: e.max(out=CVv[:, o8:o8 + 8], in_=cd), r=["W1"], w=["CVv"])
                    K.op("dve", lambda e, cd=cd, o8=o8: e.max_index(out=CI[:, o8:o8 + 8], in_max=CVv[:, o8:o8 + 8], in_values=cd),
                         r=["W1", "CVv"], w=["CI"])
                    if half == 0:
                        K.op("dve", lambda e, cd=cd, o8=o8: e.match_replace(out=cd, in_to_replace=CVv[:, o8:o8 + 8],
                                                                            in_values=cd, imm_value=-1e30),
                             r=["W1", "CVv"], w=["W1"])
            K.op("dve", lambda e: e.tensor_copy(out=JF_[:], in_=CI[:]), r=["CI"], w=["JF"])
            ge = W[2][:, :].rearrange("p (a i) -> p a i", i=16)
            K.op("dve", lambda e: e.tensor_tensor(
                out=ge, in0=JF_[:, :].unsqueeze(2).broadcast_to([128, 128, 16]),
                in1=iota16x.unsqueeze(1).broadcast_to([128, 128, 16]), op=ALU.is_ge), r=["JF", "cst"], w=["W2"])
            K.op("dve", lambda e: e.tensor_reduce(out=IF_[:], in_=ge, axis=AX.X, op=ALU.add), r=["W2"], w=["IF"])
            K.op("dve", lambda e: e.tensor_scalar(out=IF_[:], in0=IF_[:], scalar1=-1.0, scalar2=None, op0=ALU.add), r=["IF"], w=["IF"])
            K.op("dve", lambda e: e.scalar_tensor_tensor(out=JF_[:], in0=IF_[:], scalar=-16.0, in1=JF_[:], op0=ALU.mult, op1=ALU.add),
                 r=["IF", "JF"], w=["JF"])
            sif4 = SIF[:, :].rearrange("p (h q k) -> p h q k", h=8, q=2)
            for (src, q, dst, dn) in ((IF_, 0, E1, "E1"), (JF_, 1, E2, "E2")):
                eq = W[2][:, :].rearrange("p (a i) -> p a i", i=16)
                K.op("dve", lambda e, src=src, eq=eq: e.tensor_tensor(
                    out=eq, in0=iota16.unsqueeze(1).broadcast_to([128, 128, 16]),
                    in1=src[:, :].unsqueeze(2).broadcast_to([128, 128, 16]), op=ALU.is_equal),
                    r=["cst", "IF", "JF"], w=["W2"])
                eq4 = W[2][:, :].rearrange("p (h k i) -> p h k i", h=8, k=16)
                K.op("dve", lambda e, eq4=eq4, q=q: e.tensor_tensor(
                    out=eq4, in0=eq4, in1=sif4[:, :, q, :].unsqueeze(2).broadcast_to([128, 8, 16, 16]), op=ALU.mult),
                    r=["W2", "SIF"], w=["W2"])
                K.op("dve", lambda e, eq=eq, dst=dst: e.tensor_reduce(out=dst[:], in_=eq, axis=AX.X, op=ALU.add),
                     r=["W2"], w=[dn])
            K.op("dve", lambda e: e.scalar_tensor_tensor(out=E1[:], in0=E1[:], scalar=128.0, in1=E2[:], op0=ALU.mult, op1=ALU.add),
                 r=["E1", "E2"], w=["E1"])
            K.op("dve", lambda e: e.tensor_copy(out=EIDX[:], in_=E1[:]), r=["E1"], w=[en])
            cv3 = CVv[:, :].rearrange("p (h k) -> p h k", h=8)
            g3 = GATE[:, :].rearrange("p (h k) -> p h k", h=8)
            K.op("dve", lambda e: e.tensor_tensor(out=g3, in0=cv3, in1=cv3[:, :, 0:1].broadcast_to([128, 8, 16]), op=ALU.subtract),
                 r=["CVv"], w=[gn])
            K.op("act", lambda e: e.activation(out=GATE[:], in_=GATE[:], func=AF.Exp), r=[gn], w=[gn])
            K.op("dve", lambda e: e.tensor_reduce(out=sm[:, 8:16], in_=g3, axis=AX.X, op=ALU.add), r=[gn], w=["sm"])
            K.op("dve", lambda e: e.reciprocal(out=sm[:, 16:24], in_=sm[:, 8:16]), r=["sm"], w=["sm"])
            K.op("dve", lambda e: e.tensor_tensor(out=g3, in0=g3, in1=sm[:, 16:24].unsqueeze(2).broadcast_to([128, 8, 16]), op=ALU.mult),
                 r=[gn, "sm"], w=[gn])
        def peer_gather(l, Xt, Xn, gp):
            EIDX, GATE, XNB = EIDXs[gp], GATEs[gp], XNBs[gp]
            en, gn, xnbn = f"EIDX{gp}", f"GATE{gp}", f"XNB{gp}"
            gst = peer_gst
            uvr = [f"uvb{l}_{g}" for g in range(NB)]
            pend = []
            for hk in range(128):
                K.mark()
                s = gst["n"] % NB
                gst["n"] += 1
                K.dma("pool", lambda e, s=s, hk=hk: e.indirect_dma_start(
                    out=G[s][:], out_offset=None, in_=uvb_d[l],
                    in_offset=bass.IndirectOffsetOnAxis(ap=EIDX[:, hk:hk + 1], axis=0)),
                    f"g{s}", r=[en] + uvr, w=[f"G{s}"])
                K.op("dve", lambda e, s=s, hk=hk: e.scalar_tensor_tensor(
                    out=G[s][:, 0:1024], in0=G[s][:, 0:1024], scalar=1.0, in1=XNB[:], op0=ALU.mult, op1=ALU.mult,
                    accum_out=APRE[:, hk:hk + 1]), r=[f"G{s}", xnbn], w=[f"G{s}", "APRE"])
                pend.append((hk, s))
                if len(pend) == 2:
                    h0 = pend[0][0]
                    K.op("act", lambda e, h0=h0: e.activation(out=WGT[:, h0:h0 + 2], in_=APRE[:, h0:h0 + 2], func=AF.Gelu),
                         r=["APRE"], w=["WGT"])
                    for (h2, s2) in pend:
                        K.op("act", lambda e, h2=h2: e.activation(out=WGT[:, h2:h2 + 1], in_=WGT[:, h2:h2 + 1], func=AF.Copy,
                                                                 scale=GATE[:, h2:h2 + 1]), r=["WGT", gn], w=["WGT"])
                        d = gst["d"] % 8
                        gst["d"] += 1
                        K.op("act", lambda e, h2=h2, d=d: e.activation(out=DG[d][:], in_=identb[:], func=AF.Copy, scale=WGT[:, h2:h2 + 1]),
                             r=["WGT", "identb"], w=[f"DG{d}"])
                        for hf in range(2):
                            K.op("pe", lambda e, h2=h2, s2=s2, d=d, hf=hf: e.matmul(
                                PS[6 + hf][:, :], lhsT=DG[d][:], rhs=G[s2][:, 1024 + hf * 512:1536 + hf * 512],
                                start=(h2 == 0), stop=(h2 == 127)), r=[f"DG{d}", f"G{s2}"], w=[pb[6 + hf]])
                    pend = []
            K.mark()
            for hf in range(2):
                K.op("dve", lambda e, hf=hf: e.tensor_tensor(out=Xt[:, hf * 512:(hf + 1) * 512], in0=PS[6 + hf][:, :],
                                                            in1=Xt[:, hf * 512:(hf + 1) * 512], op=ALU.add),
                     r=[pb[6 + hf], Xn], w=[Xn])

        peer_gst = {"n": 0, "d": 0}

        def layer0(t, Xt, Xn, par):
            sample = (t == NT)
            first = (t == 0)
            last = (t == NT - 1)
            Uc, Up = U[par], U[1 - par]
            Ucn, Upn = f"U{par}", f"U{1 - par}"
            KTc, KTp = KT[par], KT[1 - par]
            VBc, VBp = VB[par], VB[1 - par]
            kcn, kpn, vcn, vpn = f"KT{par}", f"KT{1 - par}", f"VB{par}", f"VB{1 - par}"
            if first:
                K.op("dve", lambda e: e.memset(Up[:], 0.0), w=[Upn])
                K.op("dve", lambda e: e.memset(KTp[:], 0.0), w=[kpn])
                K.op("dve", lambda e: e.memset(VBp[:], 0.0), w=[vpn])
            if sample:
                K.op("dve", lambda e: e.memset(Up[:], 0.0), w=[Upn])
                K.dma("sp", lambda e: e.dma_start(out=Up[126:128, :], in_=cconv_d), "cc", w=[Upn])
                K.dma("sp", lambda e: e.dma_start(out=CK[:], in_=ck_d), "ck", w=["CK"])
                K.dma("sp", lambda e: e.dma_start(out=CV[:], in_=cv_d), "cv", w=["CV"])
                K.op("pe", lambda e: e.transpose(PS[1][:, 0:128], CK[:], ident), r=["CK", "cst"], w=[pb[1]])
                K.op("act", lambda e: e.activation(out=KTp[:], in_=PS[1][:, 0:128], func=AF.Copy), r=[pb[1]], w=[kpn])
                K.op("act", lambda e: e.activation(out=VBp[:], in_=CV[:], func=AF.Copy), r=["CV"], w=[vpn])
                K.dma("sp", lambda e: e.dma_start(out=ks_d[0:112, :], in_=CK[16:128, :]), "o_ks", r=["CK"], w=["o_ks"])
                K.dma("sp", lambda e: e.dma_start(out=vs_d[0:112, :], in_=CV[16:128, :]), "o_vs", r=["CV"], w=["o_vs"])
            rmsnorm(Xt[:], Xn, None, None, XN[:], "XN")
            transpose8(XN, "XN", XT, "XT", bf=True)
            bg = W[0][:, 0:512]
            cg = W[0][:, 512:1024]
            qk = W[0][:, 1024:1664]

            def cons(ci, c0, wc, b):
                if ci == 0:
                    K.op("act", lambda e: e.activation(out=bg, in_=PS[b][:, :], func=AF.Copy), r=[pb[b]], w=["W0"])
                elif ci == 1:
                    K.op("act", lambda e: e.activation(out=cg, in_=PS[b][:, :], func=AF.Copy), r=[pb[b]], w=["W0"])
                elif ci == 2:
                    K.op("dve", lambda e: e.tensor_tensor(out=Uc[:], in0=PS[b][:, :], in1=cg, op=ALU.mult),
                         r=[pb[b], "W0"], w=[Ucn])
                elif ci == 3:
                    K.op("act", lambda e: e.activation(out=qk[:, 0:512], in_=PS[b][:, :], func=AF.Copy), r=[pb[b]], w=["W0"])
                else:
                    K.op("act", lambda e: e.activation(out=qk[:, 512:640], in_=PS[b][:, 0:128], func=AF.Copy), r=[pb[b]], w=["W0"])
                    K.op("act", lambda e: e.activation(out=VF[:], in_=PS[b][:, 128:256], func=AF.Copy), r=[pb[b]], w=["VF"])
                    K.op("act", lambda e: e.activation(out=VBc[:], in_=PS[b][:, 128:256], func=AF.Copy), r=[pb[b]], w=[vcn])

            dense(XT, "XT", "w_in", cons)
            for (bank, cc, cp) in ((0, C_S1C, C_S1P), (1, C_S2C, C_S2P)):
                K.op("pe", lambda e, bank=bank, cp=cp: e.matmul(PS[bank][:, :], lhsT=cst[:, cp:cp + 128], rhs=Up[:], start=True, stop=False),
                     r=["cst", Upn], w=[pb[bank]])
                K.op("pe", lambda e, bank=bank, cc=cc: e.matmul(PS[bank][:, :], lhsT=cst[:, cc:cc + 128], rhs=Uc[:], start=False, stop=True),
                     r=["cst", Ucn], w=[pb[bank]])
            t0 = W[1][:, 1024:1536]
            t1 = W[1][:, 1536:2048]
            mixin = MIX[:, :]
            K.op("dve", lambda e: e.tensor_tensor(out=t0, in0=Uc[:], in1=convw[:, 2, :], op=ALU.mult), r=[Ucn, "convw"], w=["W1"])
            K.op("dve", lambda e: e.tensor_tensor(out=t1, in0=PS[0][:, :], in1=convw[:, 1, :], op=ALU.mult), r=[pb[0], "convw"], w=["W1"])
            K.op("dve", lambda e: e.tensor_tensor(out=t0, in0=t0, in1=t1, op=ALU.add), r=["W1"], w=["W1"])
            K.op("dve", lambda e: e.tensor_tensor(out=t1, in0=PS[1][:, :], in1=convw[:, 0, :], op=ALU.mult), r=[pb[1], "convw"], w=["W1"])
            K.op("dve", lambda e: e.tensor_tensor(out=t0, in0=t0, in1=t1, op=ALU.add), r=["W1"], w=["W1"])
            K.op("dve", lambda e: e.tensor_tensor(out=mixin[:, 0:512], in0=t0, in1=bg, op=ALU.mult), r=["W1", "W0"], w=["MIX"])
            sq = W[1][:, 0:640]
            K.op("act", lambda e: e.activation(out=sq, in_=qk, func=AF.Square), r=["W0"], w=["W1"])
            K.op("dve", lambda e: e.tensor_reduce(out=sm[:, 32:42], in_=sq.rearrange("p (h d) -> p h d", d=64), axis=AX.X, op=ALU.add),
                 r=["W1"], w=["sm"])
            K.op("act", lambda e: e.activation(out=sm[:, 48:58], in_=sm[:, 32:42], func=AF.Sqrt, scale=1.0 / 64, bias=EPS), r=["sm"], w=["sm"])
            K.op("dve", lambda e: e.reciprocal(out=sm[:, 64:74], in_=sm[:, 48:58]), r=["sm"], w=["sm"])
            q3 = QKN[:, :].rearrange("p (h d) -> p h d", d=64)
            K.op("dve", lambda e: e.tensor_tensor(out=q3, in0=qk.rearrange("p (h d) -> p h d", d=64),
                                                  in1=sm[:, 64:74].unsqueeze(2).broadcast_to([128, 10, 64]), op=ALU.mult),
                 r=["W0", "sm"], w=["QKN"])
            K.op("dve", lambda e: e.tensor_tensor(out=QKN[:], in0=QKN[:], in1=qkg[:], op=ALU.mult), r=["QKN", "qkg"], w=["QKN"])
            for i in range(4):
                K.op("pe", lambda e, i=i: e.transpose(PS[0][:, i * 128:(i + 1) * 128], QKN[:, i * 128:(i + 1) * 128], ident),
                     r=["QKN", "cst"], w=[pb[0]])
            K.op("pe", lambda e: e.transpose(PS[1][:, 0:128], QKN[:, 512:640], ident), r=["QKN", "cst"], w=[pb[1]])
            QT = H[0]
            K.op("act", lambda e: e.activation(out=QT[:, 0:512], in_=PS[0][:, :], func=AF.Copy), r=[pb[0]], w=["H0"])
            K.op("act", lambda e: e.activation(out=KTc[:], in_=PS[1][:, 0:128], func=AF.Copy), r=[pb[1]], w=[kcn])
            def sloc(j):
                r_, i_ = j % 2, j // 2
                return 2 + r_ * 2 + i_ // 2, (i_ % 2) * 256
            for j in range(8):
                b, co = sloc(j)
                r_ = j % 2
                lq = QT[r_ * 64:(r_ + 1) * 64, (j // 2) * 128:(j // 2 + 1) * 128]
                K.op("pe", lambda e, b=b, co=co, lq=lq, r_=r_: e.matmul(PS[b][:, co:co + 128], lhsT=lq, rhs=KTp[r_ * 64:(r_ + 1) * 64, :],
                                                                     start=True, stop=True), r=["H0", kpn], w=[pb[b]])
                K.op("pe", lambda e, b=b, co=co, lq=lq, r_=r_: e.matmul(PS[b][:, co + 128:co + 256], lhsT=lq, rhs=KTc[r_ * 64:(r_ + 1) * 64, :],
                                                                     start=True, stop=True), r=["H0", kcn], w=[pb[b]])
            mi = 2 if sample else (0 if first else 1)
            Ssb = W[2]
            for b in range(2, 6):
                heads = [j for j in range(8) if sloc(j)[0] == b]
                heads.sort(key=lambda j: sloc(j)[1])
                sl = Ssb[:, (b - 2) * 512:(b - 1) * 512]
                K.op("dve", lambda e, b=b, sl=sl: e.tensor_tensor(out=sl, in0=PS[b][:, :], in1=msk[:, mi, :], op=ALU.add),
                     r=[pb[b], "msk"], w=["W2"])
                mx = sm[:, 80 + 2 * (b - 2):82 + 2 * (b - 2)]
                K.op("dve", lambda e, sl=sl, mx=mx: e.tensor_reduce(out=mx, in_=sl.rearrange("p (h k) -> p h k", h=2), axis=AX.X, op=ALU.max),
                     r=["W2"], w=["sm"])
                for hh, j in enumerate(heads):
                    m1 = sm[:, 80 + 2 * (b - 2) + hh:81 + 2 * (b - 2) + hh]
                    nm = sm[:, 96 + j:97 + j]
                    K.op("dve", lambda e, m1=m1, nm=nm, j=j: e.tensor_scalar(out=nm, in0=m1, scalar1=sinkr[:, j:j + 1], scalar2=-1.0,
                                                                            op0=ALU.max, op1=ALU.mult), r=["sm", "sinkr"], w=["sm"])
                    K.op("act", lambda e, sl=sl, hh=hh, nm=nm, j=j: e.activation(
                        out=PB[:, j, :], in_=sl[:, hh * 256:(hh + 1) * 256], func=AF.Exp, bias=nm, scale=1.0,
                        accum_out=sm[:, 112 + j:113 + j]), r=["W2", "sm"], w=["PB", "sm"])
            K.op("dve", lambda e: e.tensor_tensor(out=sm[:, 128:136], in0=sinkr[:], in1=sm[:, 96:104], op=ALU.add), r=["sm", "sinkr"], w=["sm"])
            K.op("act", lambda e: e.activation(out=sm[:, 128:136], in_=sm[:, 128:136], func=AF.Exp), r=["sm"], w=["sm"])
            K.op("dve", lambda e: e.tensor_tensor(out=sm[:, 128:136], in0=sm[:, 128:136], in1=sm[:, 112:120], op=ALU.add), r=["sm"], w=["sm"])
            K.op("dve", lambda e: e.reciprocal(out=sm[:, 136:144], in_=sm[:, 128:136]), r=["sm"], w=["sm"])
            for j in range(8):
                for kb in range(2):
                    n = j * 2 + kb
                    bank = n // 8
                    pv = PS[bank][:, :].bitcast(BF)
                    K.op("pe", lambda e, pv=pv, n=n, j=j, kb=kb: e.transpose(
                        pv[:, (n % 8) * 128:(n % 8 + 1) * 128], PB[:, j, kb * 128:(kb + 1) * 128], identb[:]),
                        r=["PB", "identb"], w=[pb[bank]])
            for bank in (0, 1):
                pv = PS[bank][:, :].bitcast(BF)
                K.op("act", lambda e, pv=pv, bank=bank: e.activation(
                    out=PT[:, bank * 8:(bank + 1) * 8, :], in_=pv.rearrange("p (a n) -> p a n", a=8), func=AF.Copy),
                    r=[pb[bank]], w=["PT"])
            for j in range(8):
                g = j % 2
                K.op("pe", lambda e, j=j, g=g: e.matmul(PS[2][:, j * 64:(j + 1) * 64], lhsT=PT[:, 2 * j, :], rhs=VBp[:, g * 64:(g + 1) * 64],
                                                     start=True, stop=False), r=["PT", vpn], w=[pb[2]])
                K.op("pe", lambda e, j=j, g=g: e.matmul(PS[2][:, j * 64:(j + 1) * 64], lhsT=PT[:, 2 * j + 1, :], rhs=VBc[:, g * 64:(g + 1) * 64],
                                                     start=False, stop=True), r=["PT", vcn], w=[pb[2]])
            K.op("dve", lambda e: e.tensor_tensor(
                out=mixin[:, 512:1024].rearrange("p (h d) -> p h d", d=64), in0=PS[2][:, :].rearrange("p (h d) -> p h d", d=64),
                in1=sm[:, 136:144].unsqueeze(2).broadcast_to([128, 8, 64]), op=ALU.mult), r=[pb[2], "sm"], w=["MIX"])
            if last or sample:
                if sample:
                    K.dma("sp", lambda e: e.dma_start(out=convs_d, in_=Uc[14:16, :]), "o_cs", r=[Ucn], w=["o_cs"])
                    K.dma("sp", lambda e: e.dma_start(out=ks_d[112:128, :], in_=QKN[0:16, 512:640]), "o_ks2", r=["QKN"], w=["o_ks2"])
                    K.dma("sp", lambda e: e.dma_start(out=vs_d[112:128, :], in_=VF[0:16, :]), "o_vs2", r=["VF"], w=["o_vs2"])
                else:
                    K.dma("sp", lambda e: e.dma_start(out=convp_d, in_=Uc[126:128, :]), "o_cp", r=[Ucn], w=["o_cp"])
                    K.dma("sp", lambda e: e.dma_start(out=kp_d, in_=QKN[:, 512:640]), "o_kp", r=["QKN"], w=["o_kp"])
                    K.dma("sp", lambda e: e.dma_start(out=vp_d, in_=VF[:]), "o_vp", r=["VF"], w=["o_vp"])
            transpose8(MIX, "MIX", XT, "XT")

            def cons2(ci, c0, wc, b):
                K.op("dve", lambda e: e.tensor_tensor(out=Xt[:, c0:c0 + wc], in0=PS[b][:, :], in1=Xt[:, c0:c0 + wc], op=ALU.add),
                     r=[pb[b], Xn], w=[Xn])
            dense(XT, "XT", "w_out0", cons2)

        def layer1(t, Xt, Xn, par):
            sample = (t == NT)
            first = (t == 0)
            last = (t == NT - 1)
            if first:
                K.op("dve", lambda e: e.memset(S32[:], 0.0), w=["S32"])
                K.op("dve", lambda e: e.memset(SBF[par][0][:], 0.0), w=[f"SBF{par}0"])
            if sample:
                K.dma("sp", lambda e: e.dma_start(out=S32[:], in_=shg_d.rearrange("h d e -> d h e")), "shg", w=["S32"])
                K.op("act", lambda e: e.activation(out=SBF[par][0][:], in_=S32[:], func=AF.Copy), r=["S32"], w=[f"SBF{par}0"])
            rmsnorm(Xt[:], Xn, None, None, XN[:], "XN")
            transpose8(XN, "XN", XT, "XT", bf=True)
            qs = W[0][:, 0:1024]
            lf = W[0][:, 1024:2048]
            kk = W[1][:, 0:1024]
            sg = W[1][:, 1024:2048]
            tmp = W[2][:, 0:1024]
            vv = H[0]

            def cons(ci, c0, wc, b):
                h = (ci % 2) * 512
                if ci < 2:
                    K.op("act", lambda e: e.activation(out=qs[:, h:h + 512], in_=PS[b][:, :], func=AF.Copy), r=[pb[b]], w=["W0"])
                elif ci < 4:
                    K.op("act", lambda e: e.activation(out=tmp[:, h:h + 512], in_=PS[b][:, :], func=AF.Sigmoid, scale=-1.0), r=[pb[b]], w=["W2"])
                    K.op("dve", lambda e: e.tensor_tensor(out=kk[:, h:h + 512], in0=tmp[:, h:h + 512], in1=omlr[:, h:h + 512], op=ALU.mult),
                         r=["W2", "omlr"], w=["W1"])
                    K.op("act", lambda e: e.activation(out=lf[:, h:h + 512], in_=kk[:, h:h + 512], func=AF.Ln, scale=-1.0, bias=1.0), r=["W1"], w=["W0"])
                    if sample:
                        K.op("dve", lambda e: e.tensor_scalar(out=lf[:, h:h + 512], in0=lf[:, h:h + 512], scalar1=validc, scalar2=None, op0=ALU.mult),
                             r=["W0", "cst"], w=["W0"])
                        K.op("dve", lambda e: e.tensor_scalar(out=kk[:, h:h + 512], in0=kk[:, h:h + 512], scalar1=validc, scalar2=None, op0=ALU.mult),
                             r=["W1", "cst"], w=["W1"])
                elif ci < 6:
                    K.op("act", lambda e: e.activation(out=vv[:, h:h + 512], in_=PS[b][:, :], func=AF.Copy), r=[pb[b]], w=["H0"])
                else:
                    K.op("act", lambda e: e.activation(out=sg[:, h:h + 512], in_=PS[b][:, :], func=AF.Silu), r=[pb[b]], w=["W1"])

            dense(XT, "XT", "hw_in", cons)
            for hf in range(2):
                K.op("pe", lambda e, hf=hf: e.matmul(PS[4 + hf][:, :], lhsT=tri, rhs=lf[:, hf * 512:(hf + 1) * 512], start=True, stop=True),
                     r=["cst", "W0"], w=[pb[4 + hf]])
                K.op("pe", lambda e, hf=hf: e.matmul(PS[hf][:, :], lhsT=rev, rhs=lf[:, hf * 512:(hf + 1) * 512], start=True, stop=True),
                     r=["cst", "W0"], w=[pb[hf]])
            for h in range(8):
                K.op("pe", lambda e, h=h: e.matmul(PS[2][:, 2 * h:2 * h + 2], lhsT=lf[:, h * 128:(h + 1) * 128], rhs=blkind, start=True, stop=True),
                     r=["cst", "W0"], w=[pb[2]])
            K.op("act", lambda e: e.activation(out=DEC[:], in_=PS[2][:, 0:16], func=AF.Exp), r=[pb[2]], w=["DEC"])
            qt, kt, kd = H[1], H[2], H[3]
            ex = W[2][:, 1024:2048]
            for hf in range(2):
                K.op("act", lambda e, hf=hf: e.activation(out=ex[:, hf * 512:(hf + 1) * 512], in_=PS[4 + hf][:, :], func=AF.Exp), r=[pb[4 + hf]], w=["W2x"])
            K.op("dve", lambda e: e.tensor_tensor(out=qt[:], in0=qs, in1=ex, op=ALU.mult), r=["W0", "W2x"], w=["H1"])
            for hf in range(2):
                K.op("act", lambda e, hf=hf: e.activation(out=ex[:, hf * 512:(hf + 1) * 512], in_=PS[4 + hf][:, :], func=AF.Exp, scale=-1.0),
                     r=[pb[4 + hf]], w=["W2x"])
            K.op("dve", lambda e: e.tensor_tensor(out=kt[:], in0=kk, in1=ex, op=ALU.mult), r=["W1", "W2x"], w=["H2"])
            for hf in range(2):
                K.op("act", lambda e, hf=hf: e.activation(out=ex[:, hf * 512:(hf + 1) * 512], in_=PS[hf][:, :], func=AF.Exp), r=[pb[hf]], w=["W2x"])
            K.op("dve", lambda e: e.tensor_tensor(out=kd[:], in0=kk, in1=ex, op=ALU.mult), r=["W1", "W2x"], w=["H3"])
            PBf = PB[:, :, :].rearrange("p a b -> p (a b)")
            for (src, sn, bank, off) in ((qt, "H1", 3, 0), (kt, "H2", 4, 1024)):
                pv = PS[bank][:, :].bitcast(BF)
                for h in range(8):
                    K.op("pe", lambda e, pv=pv, src=src, h=h: e.transpose(pv[:, h * 128:(h + 1) * 128], src[:, h * 128:(h + 1) * 128], identb[:]),
                         r=[sn, "identb"], w=[pb[bank]])
                K.op("act", lambda e, pv=pv, off=off: e.activation(out=PBf[:, off:off + 1024], in_=pv, func=AF.Copy), r=[pb[bank]], w=["PB"])
            qtT = PBf[:, 0:1024]
            ktT = PBf[:, 1024:2048]
            for h in range(8):
                for blk in range(2):
                    n = h * 2 + blk
                    bank = (n // 4) % 2
                    co = (n % 4) * 128
                    K.op("pe", lambda e, bank=bank, co=co, h=h, blk=blk: e.matmul(
                        PS[bank][:, co:co + 128], lhsT=kd[blk * 64:(blk + 1) * 64, h * 128:(h + 1) * 128],
                        rhs=vv[blk * 64:(blk + 1) * 64, h * 128:(h + 1) * 128], start=True, stop=True),
                        r=["H3", "H0"], w=[pb[bank]])
                    K.op("dve", lambda e, bank=bank, co=co, h=h, blk=blk: e.scalar_tensor_tensor(
                        out=S32[:, h, :], in0=S32[:, h, :], scalar=DEC[:, 2 * h + blk:2 * h + blk + 1], in1=PS[bank][:, co:co + 128],
                        op0=ALU.mult, op1=ALU.add), r=["S32", "DEC", pb[bank]], w=["S32"])
                    dstb = SBF[par][1] if blk == 0 else SBF[1 - par][0]
                    dn = f"SBF{par}1" if blk == 0 else f"SBF{1 - par}0"
                    K.op("act", lambda e, dstb=dstb, h=h: e.activation(out=dstb[:, h, :], in_=S32[:, h, :], func=AF.Copy), r=["S32"], w=[dn])
            for h in range(8):
                bank = (2, 5)[h // 4]
                K.op("pe", lambda e, bank=bank, h=h: e.matmul(PS[bank][:, (h % 4) * 128:(h % 4 + 1) * 128], lhsT=ktT[:, h * 128:(h + 1) * 128],
                                                           rhs=qtT[:, h * 128:(h + 1) * 128], start=True, stop=True), r=["PB"], w=[pb[bank]])
            for g in range(2):
                K.op("dve", lambda e, g=g: e.tensor_tensor(out=PT[:, 4 * g:4 * g + 4, :].rearrange("p a b -> p (a b)"), in0=PS[(2, 5)[g]][:, :],
                                                          in1=msk[:, 3, :], op=ALU.mult), r=[pb[(2, 5)[g]], "msk"], w=["PT"])
            for h in range(8):
                bank = 3 + h // 4
                co = (h % 4) * 128
                K.op("pe", lambda e, bank=bank, co=co, h=h: e.matmul(PS[bank][:, co:co + 128], lhsT=PT[:, h, :], rhs=vv[:, h * 128:(h + 1) * 128],
                                                                  start=True, stop=False), r=["PT", "H0"], w=[pb[bank]])
                for blk in range(2):
                    K.op("pe", lambda e, bank=bank, co=co, h=h, blk=blk: e.matmul(
                        PS[bank][blk * 64:(blk + 1) * 64, co:co + 128], lhsT=qtT[:, h * 128 + blk * 64:h * 128 + (blk + 1) * 64],
                        rhs=SBF[par][blk][:, h, :], start=False, stop=True),
                        r=["PB", f"SBF{par}{blk}"], w=[pb[bank]])
            osb = W[2][:, 0:1024]
            for g in range(2):
                K.op("act", lambda e, g=g: e.activation(out=osb[:, g * 512:(g + 1) * 512], in_=PS[3 + g][:, :], func=AF.Copy), r=[pb[3 + g]], w=["W2"])
            hm = MIX[:, :]
            rmsnorm(osb, "W2", sg, "W1", hm, "MIX")
            if last or sample:
                od = ss_d if sample else sp_d
                K.dma("sp", lambda e: e.dma_start(out=od.rearrange("h d e -> d h e"), in_=S32[:]), "o_s" + str(int(sample)), r=["S32"], w=["o_s" + str(int(sample))])
            transpose8(hm, "MIX", XT, "XT")

            def cons2(ci, c0, wc, b):
                K.op("dve", lambda e: e.tensor_tensor(out=Xt[:, c0:c0 + wc], in0=PS[b][:, :], in1=Xt[:, c0:c0 + wc], op=ALU.add),
                     r=[pb[b], Xn], w=[Xn])
            dense(XT, "XT", "hw_out", cons2)

        tiles = list(range(NT + 1))

        def Xof(t):
            return X[t % 3], f"X{t % 3}"

        def L0(t, gp):
            Xt, Xn = Xof(t)
            if t < NT:
                K.dma("sp", lambda e: e.dma_start(out=Xt[:], in_=xp_d[t * 128:(t + 1) * 128, :]), f"x{t % 3}", w=[Xn])
            else:
                K.op("dve", lambda e: e.memset(Xt[:], 0.0), w=[Xn])
                K.dma("sp", lambda e: e.dma_start(out=Xt[0:16, :], in_=xs_d), f"x{t % 3}", w=[Xn])
            layer0(t, Xt, Xn, t % 2)
            peer_topk(0, Xt, Xn, gp)

        def L1(t, gp):
            Xt, Xn = Xof(t)
            layer1(t, Xt, Xn, t % 2)
            peer_topk(1, Xt, Xn, gp)

        def G0(t, gp):
            Xt, Xn = Xof(t)
            peer_gather(0, Xt, Xn, gp)

        def G1(t, gp):
            Xt, Xn = Xof(t)
            peer_gather(1, Xt, Xn, gp)
            if t < NT:
                K.dma("sp", lambda e: e.dma_start(out=yp_d[t * 128:(t + 1) * 128, :], in_=Xt[:]), f"y{t % 3}", r=[Xn], w=[f"oy{t % 3}"])
            else:
                K.dma("sp", lambda e: e.dma_start(out=ys_d, in_=Xt[0:16, :]), f"y{t % 3}", r=[Xn], w=[f"oy{t % 3}"])

        Lchain, Gchain = [], []
        i = 0
        while i < len(tiles):
            pair = tiles[i:i + 2]
            i += 2
            for t in pair:
                Lchain.append((L0, t))
            for t in pair:
                Lchain.append((L1, t))
            for t in pair:
                Gchain.append((G0, t))
            for t in pair:
                Gchain.append((G1, t))
        for (f_, t) in Lchain:
            names = ("w_in", "w_out0", "wq0") if f_ is L0 else ("hw_in", "hw_out", "wq1")
            for name in names:
                wseq.extend([(name, c) for c in ctab[name]])

        def record(f_, t, gp):
            K.rec = []
            f_(t, gp)
            r_ = K.rec
            K.rec = None
            return r_

        def emit_alone(recs):
            for r_ in recs:
                K.process(r_)

        emit_alone(record(Lchain[0][0], Lchain[0][1], 0))
        li = 1
        odd_tail = (len(tiles) % 2 == 1)
        for k in range(len(Gchain)):
            gf, gt = Gchain[k]
            grec = record(gf, gt, k % 2)
            import os
            if os.environ.get('PIPE', '1') == '1' and li < len(Lchain) and not (Lchain[li][0] is L1 and gf is G0 and Lchain[li][1] == gt) and not (odd_tail and Lchain[li][1] == tiles[-1]):
                lrec = record(Lchain[li][0], Lchain[li][1], li % 2)
                li += 1
                K.merge(grec, lrec)
            else:
                emit_alone(grec)
            while li <= k + 1 and li < len(Lchain):
                emit_alone(record(Lchain[li][0], Lchain[li][1], li % 2))
                li += 1
        K.wait_all("sp")
        K.emit()
        print("merge_stats", getattr(K, "merge_stats", None))
        print("n_ops", K.n_ops, {e: len(v) for e, v in K.ops.items()})
    return nc


def host_consts():
    c = np.zeros((128, C_N), np.float32)
    s = np.arange(128)[:, None]
    t = np.arange(128)[None, :]
    same = (s // 64) == (t // 64)
    c[:, C_ID:C_ID + 128] = np.eye(128)
    c[:, C_TRI:C_TRI + 128] = (same & (s <= t))
    c[:, C_REV:C_REV + 128] = (same & (s > t))
    c[:, C_S1C:C_S1C + 128] = (s == t - 1)
    c[:, C_S1P:C_S1P + 128] = (s == t - 1 + 128)
    c[:, C_S2C:C_S2C + 128] = (s == t - 2)
    c[:, C_S2P:C_S2P + 128] = (s == t - 2 + 128)
    c[:, C_BLK] = (np.arange(128) < 64)
    c[:, C_BLK + 1] = (np.arange(128) >= 64)
    c[:, C_IOTA:C_IOTA + 16] = np.arange(16)[None, :]
    c[:, C_VALID] = (np.arange(128) < 16)
    c[:, C_IOTA16:C_IOTA16 + 16] = 16.0 * np.arange(16)[None, :]
    m = np.zeros((128, 4, 2, 256), np.float32)
    a = np.zeros((128, 256), np.float32)
    a[0:64, 192:256] = NEG
    a[64:128, 0:64] = NEG
    a0 = a.copy()
    a0[:, 0:128] = NEG
    ms = np.zeros((128, 256), np.float32)
    ms[:, 144:256] = NEG
    m[:, 0] = a0[:, None, :]
    m[:, 1] = a[:, None, :]
    m[:, 2] = ms[:, None, :]
    m = m.reshape(128, 4, 512)
    m[:, 3] = np.tile(c[:, C_TRI:C_TRI + 128], (1, 4))
    return c, m


def host_prep(inp, NT):
    f = lambda a: np.ascontiguousarray(np.asarray(a, dtype=np.float32))
    w_in = f(inp["even_w_in"][0]).copy()
    qcols = w_in[:, 1536:2048].reshape(1024, 8, 64)[:, QPERM, :].reshape(1024, 512)
    w_in[:, 1536:2048] = qcols
    w_out0 = f(inp["even_w_out"][0]).copy()
    w_out0[512:1024] = w_out0[512:1024].reshape(8, 64, 1024)[QPERM].reshape(512, 1024)
    sinks = f(inp["even_sinks"][0])[QPERM]
    keysT = np.ascontiguousarray(f(inp["peer_sub_keys"]).transpose(0, 4, 1, 2, 3).reshape(2, 128, 2048))
    uv = np.concatenate([f(inp["peer_u"]), f(inp["peer_v"])], axis=2)
    cst, msk = host_consts()
    rep = lambda v: np.ascontiguousarray(np.broadcast_to(v, (128,) + v.shape))
    qkg = np.concatenate([np.tile(f(inp["even_q_gain"][0]), 8), np.tile(f(inp["even_k_gain"][0]), 2)])
    gcol = np.concatenate([f(inp["norm_mix"][0]).reshape(8, 128).T, f(inp["norm_mix"][1]).reshape(8, 128).T,
                           f(inp["hgrn_out_gain"][0]).reshape(8, 128).T], axis=1)
    shared = {
        "w_in": w_in, "w_out0": w_out0, "hw_in": f(inp["hgrn_w_in"][0]), "hw_out": f(inp["hgrn_w_out"][0]),
        "wq": f(inp["peer_w_query"]), "keysT": keysT, "uv0": uv[0], "uv1": uv[1], "cst": cst, "msk": msk.astype(ml_dtypes.bfloat16),
        "nffn": rep(f(inp["norm_ffn"])).astype(ml_dtypes.bfloat16), "hlb": rep(f(inp["hgrn_lb"])), "convw": rep(f(inp["even_conv_w"][0])),
        "qkg": rep(qkg), "sinkr": rep(sinks), "gcol": np.ascontiguousarray(gcol),
    }
    maps = []
    for c in range(NCORES):
        d = dict(shared)
        d["xp"] = f(inp["x_prompt"][c][:NT * 128])
        d["xs"] = f(inp["x_sample"][c])
        d["cconv"] = f(inp["cache_conv"][0, c])
        d["ck"] = f(inp["cache_k"][0, c]).reshape(128, 128)
        d["cv"] = f(inp["cache_v"][0, c]).reshape(128, 128)
        d["shg"] = f(inp["state_hgrn"][0, c])
        maps.append(d)
    return maps


def gather_out(res, NT):
    R = res.results
    st = lambda k: np.stack([np.asarray(r[k], dtype=np.float32) for r in R])
    y_p = st("y_p")
    y_s = st("y_s")
    return (y_p, y_s,
            st("conv_p")[None], st("k_p").reshape(1, NCORES, 128, 2, 64), st("v_p").reshape(1, NCORES, 128, 2, 64), st("s_p")[None],
            st("conv_s")[None], st("k_s").reshape(1, NCORES, 128, 2, 64), st("v_s").reshape(1, NCORES, 128, 2, 64), st("s_s")[None])


def run(inp, NT, stage=99, trace=False):
    nc = build(NT, stage)
    maps = host_prep(inp, NT)
    res = run_bass_kernel_spmd(nc, maps, core_ids=list(range(NCORES)), trace=trace)
    return gather_out(res, NT), res


def kernel(**inputs):
    out, _ = run(inputs, 32)
    return out
```
